# Optimizing a Trainium2 kernel written in Bass

```python
import math
import jax, jax.numpy as jnp
from jax import lax
import numpy as np

D_MODEL = 2048
BATCH = 2
SEQ = 4096
DEPTH = 1

GRID_W = 64
NA_HEADS = 8
NA_DH = 128
NA_KR_MAX = 8
NA_KW = 16
DF_HEADS = 8
DF_DQ = 64
DF_DV = 2 * DF_DQ
Q_BLOCK = 128
FFN_HID = int(math.ceil(8 * D_MODEL / 3 / 256) * 256)
PLE_DIM = 256
ROPE_THETA = 10000.0
EPS = 1e-6

NA_W = NA_HEADS * NA_DH
DF_QW = DF_HEADS * 2 * DF_DQ
DF_VW = DF_HEADS * DF_DV
IN_COLS = 3 * NA_W + 2 * DF_QW + DF_VW + 2 * D_MODEL

kernel_name = "hybrid_natten_diffattn_gated_encoder"


def rmsnorm(x, g):
    xf = x.astype(jnp.float32)
    y = xf * lax.rsqrt(jnp.mean(xf * xf, axis=-1, keepdims=True) + EPS)
    return (y * g.astype(jnp.float32)).astype(x.dtype)


def rope_tables(S, d):
    inv = 1.0 / (ROPE_THETA ** (jnp.arange(0, d, 2, dtype=jnp.float32) / d))
    ang = jnp.arange(S, dtype=jnp.float32)[:, None] * inv[None, :]
    return jnp.cos(ang), jnp.sin(ang)


def apply_rope(x, cos, sin):
    c = cos.astype(x.dtype)
    s = sin.astype(x.dtype)
    x1, x2 = jnp.split(x, 2, axis=-1)
    return jnp.concatenate([x1 * c - x2 * s, x1 * s + x2 * c], axis=-1)


def neighbourhood_attention(q, k, v, rpb):
    B, S, H, dh = q.shape
    rows = S // GRID_W
    kr = min(NA_KR_MAX, rows)
    scale = dh ** -0.5
    qg = q.transpose(0, 2, 1, 3).reshape(B, H, rows, GRID_W, dh)
    kg = k.transpose(0, 2, 1, 3).reshape(B, H, rows, GRID_W, dh)
    vg = v.transpose(0, 2, 1, 3).reshape(B, H, rows, GRID_W, dh)
    c = np.arange(GRID_W)
    cs = np.clip(c - NA_KW // 2, 0, GRID_W - NA_KW)
    col_idx = cs[:, None] + np.arange(NA_KW)[None, :]
    dc = col_idx - c[:, None] + (NA_KW - 1)

    def one_row(r):
        rs = jnp.clip(r - kr // 2, 0, rows - kr)
        k_band = lax.dynamic_slice_in_dim(kg, rs, kr, axis=2)
        v_band = lax.dynamic_slice_in_dim(vg, rs, kr, axis=2)
        q_row = lax.dynamic_index_in_dim(qg, r, axis=2, keepdims=False)
        k_sel = k_band[:, :, :, col_idx, :]
        v_sel = v_band[:, :, :, col_idx, :]
        dr = rs + jnp.arange(kr) - r + (NA_KR_MAX - 1)
        bias = rpb[:, dr[None, :, None], dc[:, None, :]]
        s = jnp.einsum('bhcd,bhrckd->bhcrk', q_row, k_sel).astype(jnp.float32) * scale
        s = s + bias.astype(jnp.float32)[None]
        pm = jax.nn.softmax(s.reshape(B, H, GRID_W, kr * NA_KW), axis=-1)
        pm = pm.reshape(B, H, GRID_W, kr, NA_KW).astype(v.dtype)
        return jnp.einsum('bhcrk,bhrckd->bhcd', pm, v_sel)

    o = lax.map(one_row, jnp.arange(rows))
    return o.transpose(1, 0, 3, 2, 4).reshape(B, S, H * dh)


def differential_attention(q, k, v, lam):
    B, S, H, _, dq = q.shape
    nb = S // Q_BLOCK
    scale = dq ** -0.5
    q_t = q.transpose(0, 2, 3, 1, 4)
    k_t = k.transpose(0, 2, 3, 1, 4)
    v_t = v.transpose(0, 2, 1, 3)
    q_blocks = q_t.reshape(B, H, 2, nb, Q_BLOCK, dq).transpose(3, 0, 1, 2, 4, 5)

    def one_block(qb):
        s = jnp.einsum('bhmqd,bhmkd->bhmqk', qb, k_t).astype(jnp.float32) * scale
        pm = jax.nn.softmax(s, axis=-1)
        a = pm[:, :, 0] - lam * pm[:, :, 1]
        return jnp.einsum('bhqk,bhkd->bhqd', a.astype(v_t.dtype), v_t)

    o = lax.map(one_block, q_blocks)
    return o.transpose(1, 0, 3, 2, 4).reshape(B, S, H, v.shape[-1])


def setup_inputs(seed: int = 0) -> dict:
    key = jax.random.key(seed)
    ks = jax.random.split(key, 24)
    L = DEPTH
    n = lambda k, shape, s: jax.random.normal(k, shape, jnp.float32) * s
    gain = lambda k, d: 1.0 + 0.02 * jax.random.normal(k, (L, d), jnp.float32)
    return {
        "x": n(ks[0], (BATCH, SEQ, D_MODEL), 1.0),
        "p": n(ks[1], (DEPTH, BATCH, SEQ, PLE_DIM), 1.0),
        "g_mix": gain(ks[2], D_MODEL),
        "w_in": n(ks[3], (L, D_MODEL, IN_COLS), D_MODEL ** -0.5),
        "g_na_q": gain(ks[4], NA_DH),
        "g_na_k": gain(ks[5], NA_DH),
        "na_rpb": n(ks[6], (L, NA_HEADS, 2 * NA_KR_MAX - 1, 2 * NA_KW - 1), 0.1),
        "g_df_q": gain(ks[7], DF_DQ),
        "g_df_k": gain(ks[8], DF_DQ),
        "lam_q1": n(ks[9], (L, DF_DQ), 0.1),
        "lam_k1": n(ks[10], (L, DF_DQ), 0.1),
        "lam_q2": n(ks[11], (L, DF_DQ), 0.1),
        "lam_k2": n(ks[12], (L, DF_DQ), 0.1),
        "g_df_sub": gain(ks[13], DF_DV),
        "w_na_out": n(ks[14], (L, NA_W, D_MODEL), NA_W ** -0.5),
        "w_df_out": n(ks[15], (L, DF_VW, D_MODEL), DF_VW ** -0.5),
        "w_o": n(ks[16], (L, D_MODEL, D_MODEL), D_MODEL ** -0.5),
        "g_ffn": gain(ks[17], D_MODEL),
        "w_gate": n(ks[18], (L, D_MODEL, FFN_HID), D_MODEL ** -0.5),
        "w_up": n(ks[19], (L, D_MODEL, FFN_HID), D_MODEL ** -0.5),
        "w_down": n(ks[20], (L, FFN_HID, D_MODEL), FFN_HID ** -0.5),
        "g_ple": gain(ks[21], D_MODEL),
        "w_ple_gate": n(ks[22], (L, D_MODEL, D_MODEL), D_MODEL ** -0.5),
        "w_ple_proj": n(ks[23], (L, PLE_DIM, D_MODEL), PLE_DIM ** -0.5),
    }


def reference(x, p, g_mix, w_in, g_na_q, g_na_k, na_rpb, g_df_q, g_df_k,
              lam_q1, lam_k1, lam_q2, lam_k2, g_df_sub, w_na_out, w_df_out, w_o,
              g_ffn, w_gate, w_up, w_down, g_ple, w_ple_gate, w_ple_proj):
    B, S, D = x.shape
    cos, sin = rope_tables(S, DF_DQ)
    cos_b = cos[None, :, None, None, :]
    sin_b = sin[None, :, None, None, :]
    splits = np.cumsum([NA_W, NA_W, NA_W, DF_QW, DF_QW, DF_VW, D_MODEL])
    for i in range(DEPTH):
        lam_init = 0.8 - 0.6 * math.exp(-0.3 * i)
        h = rmsnorm(x, g_mix[i])
        proj = h @ w_in[i]
        na_q, na_k, na_v, df_q, df_k, df_v, gate_a, gate_b = jnp.split(proj, splits, axis=-1)

        na_q = rmsnorm(na_q.reshape(B, S, NA_HEADS, NA_DH), g_na_q[i])
        na_k = rmsnorm(na_k.reshape(B, S, NA_HEADS, NA_DH), g_na_k[i])
        na_v = na_v.reshape(B, S, NA_HEADS, NA_DH)
        y_a = neighbourhood_attention(na_q, na_k, na_v, na_rpb[i]) @ w_na_out[i]

        df_q = apply_rope(rmsnorm(df_q.reshape(B, S, DF_HEADS, 2, DF_DQ), g_df_q[i]), cos_b, sin_b)
        df_k = apply_rope(rmsnorm(df_k.reshape(B, S, DF_HEADS, 2, DF_DQ), g_df_k[i]), cos_b, sin_b)
        df_v = df_v.reshape(B, S, DF_HEADS, DF_DV)
        lam = (jnp.exp(jnp.sum(lam_q1[i].astype(jnp.float32) * lam_k1[i].astype(jnp.float32)))
               - jnp.exp(jnp.sum(lam_q2[i].astype(jnp.float32) * lam_k2[i].astype(jnp.float32)))
               + lam_init)
        o_b = differential_attention(df_q, df_k, df_v, lam)
        o_b = rmsnorm(o_b, g_df_sub[i]) * (1.0 - lam_init)
        y_b = o_b.reshape(B, S, DF_VW) @ w_df_out[i]

        merged = jax.nn.sigmoid(gate_a) * y_a + jax.nn.sigmoid(gate_b) * y_b
        x = x + merged @ w_o[i]

        hf = rmsnorm(x, g_ffn[i])
        x = x + (jax.nn.silu(hf @ w_gate[i]) * (hf @ w_up[i])) @ w_down[i]

        hp = rmsnorm(x, g_ple[i])
        x = x + jax.nn.sigmoid(hp @ w_ple_gate[i]) * (p[i] @ w_ple_proj[i])
    return x
```

```python
import contextlib
import math
import numpy as np
import concourse.bass as bass
import concourse.mybir as mybir
from concourse.bass_utils import run_bass_kernel_spmd

F32 = mybir.dt.float32
BF16 = mybir.dt.bfloat16
AF = mybir.ActivationFunctionType
ALU = mybir.AluOpType
AX = mybir.AxisListType

ENG = ["pe", "act", "dve", "pool", "sp"]

D = 2048
KC = 16
S_ALL = 4096
OWN = 1024
HID = 5632
HC = 44
EPS = 1e-6
NEG = -30000.0
C_NAQ, C_NAK, C_NAV, C_DFQ, C_DFK, C_DFV, C_GA, C_GB = 0, 1024, 2048, 3072, 4096, 5120, 6144, 8192
DEBUG = False


class Op:
    __slots__ = ("e", "idx", "fn", "deps", "dma", "flag", "cum")

    def __init__(self, e, idx, fn, deps, dma):
        self.e, self.idx, self.fn, self.deps, self.dma = e, idx, fn, deps, dma
        self.flag = False
        self.cum = 0


class Sched:
    def __init__(self, nc, stack):
        self.nc = nc
        self.esem = {e: stack.enter_context(nc.semaphore("s_" + e)) for e in ENG}
        self.psem = stack.enter_context(nc.semaphore("s_phase"))
        self.stack = stack
        self.dsem = {}
        self.dma_cnt = {}
        self.ecum = {e: 0 for e in ENG}
        self.waited = {e: {} for e in ENG}
        self.phase = 0
        self.nops = 0
        self.nwaits = 0
        self.reset()

    def reset(self):
        self.ops = {e: [] for e in ENG}
        self.buf = {}
        self.phase_dma = set()

    def add(self, e, fn, r=(), w=(), dma=None):
        idx = len(self.ops[e])
        deps = set()
        for k in r:
            st = self.buf.get(k)
            if st is not None and st[0] is not None:
                deps.add(st[0])
        for k in w:
            st = self.buf.get(k)
            if st is not None:
                if st[0] is not None:
                    deps.add(st[0])
                deps.update(st[1].values())
        if dma is not None:
            self.dma_cnt[dma] = self.dma_cnt.get(dma, 0) + 16
            ref = ("dma", dma, self.dma_cnt[dma])
            rkey = ("dma", dma)
            self.phase_dma.add(dma)
        else:
            ref = ("eng", e, idx)
            rkey = ("eng", e)
        if e == "pe":
            deps = {d for d in deps if not (d[0] == "eng" and d[1] == "pe")}
        op = Op(e, idx, fn, deps, dma)
        for k in r:
            st = self.buf.setdefault(k, [None, {}])
            st[1][rkey] = ref
        for k in w:
            self.buf[k] = [ref, {}]
        self.ops[e].append(op)
        self.nops += 1
        return op

    def emit(self):
        nc = self.nc
        for e in ENG:
            for op in self.ops[e]:
                for d in op.deps:
                    if d[0] == "eng":
                        self.ops[d[1]][d[2]].flag = True
            last = None
            for op in self.ops[e]:
                if op.dma is None:
                    last = op
            if last is not None:
                last.flag = True
        for e in ENG:
            c = self.ecum[e]
            for op in self.ops[e]:
                if op.flag and op.dma is None:
                    c += 1
                op.cum = c
            self.ecum[e] = c
        for k in sorted(self.phase_dma, key=str):
            if k not in self.dsem:
                self.dsem[k] = self.stack.enter_context(nc.semaphore("d%d" % len(self.dsem)))
        phase = self.phase

        def run(e, eng):
            waited = self.waited[e]
            if phase > 0:
                eng.wait_ge(self.psem, phase)
            for op in self.ops[e]:
                need = {}
                for d in op.deps:
                    if d[0] == "eng":
                        sem = self.esem[d[1]]
                        val = self.ops[d[1]][d[2]].cum
                        key = ("e", d[1])
                    else:
                        sem = self.dsem[d[1]]
                        val = d[2]
                        key = ("d", d[1])
                    if val > need.get(key, (None, 0))[1]:
                        need[key] = (sem, val)
                for key, (sem, val) in need.items():
                    if waited.get(key, 0) >= val:
                        continue
                    waited[key] = val
                    eng.wait_ge(sem, val)
                    self.nwaits += 1
                ins = op.fn(eng)
                if op.dma is not None:
                    ins.then_inc(self.dsem[op.dma], 16)
                elif op.flag:
                    ins.then_inc(self.esem[e], 1)
            if e == "sp":
                for e2 in ENG:
                    if e2 != "sp" and self.ecum[e2] > 0:
                        eng.wait_ge(self.esem[e2], self.ecum[e2])
                for k in sorted(self.phase_dma, key=str):
                    eng.wait_ge(self.dsem[k], self.dma_cnt[k])
                eng.nop().then_inc(self.psem, 1)

        with nc.Block() as block:
            @block.tensor
            def _(eng):
                run("pe", eng)

            @block.scalar
            def _(eng):
                run("act", eng)

            @block.vector
            def _(eng):
                run("dve", eng)

            @block.gpsimd
            def _(eng):
                run("pool", eng)

            @block.sync
            def _(eng):
                run("sp", eng)
        self.phase += 1
        self.reset()


class Rot:
    def __init__(self, items):
        self.items = list(items)
        self.i = 0

    def next(self):
        v = self.items[self.i % len(self.items)]
        self.i += 1
        return v


def build_program():
    nc = bass.Bass("TRN2", target_bir_lowering=False)
    din = lambda name, shape, dt=F32: nc.dram_tensor(name, list(shape), dt, kind="ExternalInput").ap()
    skind = "ExternalOutput" if DEBUG else "Internal"
    dscr = lambda name, shape, dt: nc.dram_tensor(name, list(shape), dt, kind=skind).ap()

    xr = din("xr", [S_ALL, D])
    pr = din("pr", [OWN, 256])
    w_in = din("w_in", [D, 10240])
    w_na_out = din("w_na_out", [1024, D])
    w_df_out = din("w_df_out", [1024, D])
    w_o = din("w_o", [D, D])
    w_gate = din("w_gate", [D, HID])
    w_up = din("w_up", [D, HID])
    w_down = din("w_down", [HID, D])
    w_pg = din("w_ple_gate", [D, D])
    w_pp = din("w_ple_proj", [256, D])
    gcols_d = din("gcols", [128, 48])
    gvec_d = din("gvec", [128, 8])
    lamv_d = din("lamv", [128, 256])
    ident_d = din("ident", [128, 128])
    rotT_d = din("rotT", [128, 128])
    ct_d = din("ropec", [128, S_ALL])
    st_d = din("ropes", [128, S_ALL])
    t2_d = din("t2", [128, 8 * 15 * 64])
    rm_d = din("rowmask", [128, 16 * 6 * 64])
    out = nc.dram_tensor("out", [OWN, D], F32, kind="ExternalOutput").ap()

    KNA = dscr("KNA", [8, 128, 1536], BF16)
    QNA = dscr("QNA", [8, 128, OWN], BF16)
    VNA = dscr("VNA", [8, 1536, 128], BF16)
    KDF = dscr("KDF", [8, 128, S_ALL], BF16)
    QDF = dscr("QDF", [8, 128, OWN], BF16)
    VDF = dscr("VDF", [8, 128, 32, 128], BF16)
    X1 = dscr("X1", [OWN, D], F32)
    X2 = dscr("X2", [OWN, D], F32)
    if DEBUG:
        OAT = dscr("OAT", [128, 8, OWN], BF16)
        OBT = dscr("OBT", [128, 8, OWN], BF16)
        MT = dscr("MT", [128, 16, OWN], BF16)

    with contextlib.ExitStack() as st0:
        S = Sched(nc, st0)

        def sb(stack, name, shape, dt):
            return stack.enter_context(nc.sbuf_tensor("sb_" + name, list(shape), dt))

        ps_all = st0.enter_context(nc.psum_tensor("ps_all", [128, 4096], F32))
        ps_bf = ps_all.bitcast(BF16)
        PS = [ps_all[:, i * 512:(i + 1) * 512] for i in range(8)]
        PSB = [ps_bf[:, i * 1024:(i + 1) * 1024] for i in range(8)]
        pk = lambda i: ("ps", i)

        ident_f = sb(st0, "ident_f", [128, 128], F32)
        rot_f = sb(st0, "rot_f", [128, 128], F32)
        ident_b = sb(st0, "ident_b", [128, 128], BF16)
        rotT_b = sb(st0, "rotT_b", [128, 128], BF16)
        ones_b = sb(st0, "ones_b", [128, 128], BF16)
        bones_b = sb(st0, "bones_b", [128, 128], BF16)
        ones_f = sb(st0, "ones_f", [128, 128], F32)
        eps_c = sb(st0, "eps_c", [128, 1], F32)
        gcols = sb(st0, "gcols", [128, 48], F32)
        gvec = sb(st0, "gvec", [128, 8], F32)
        gsub08 = sb(st0, "gsub08", [128, 1], F32)
        lamv = sb(st0, "lamv", [128, 256], F32)
        lamw = sb(st0, "lamw", [128, 128], F32)
        lams = sb(st0, "lams", [128, 4], F32)
        nlam = sb(st0, "nlam", [128, 1], F32)

        def dma(e, out_ap, in_ap, r=(), w=(), key=None):
            S.add(e, lambda eng: eng.dma_start(out=out_ap, in_=in_ap), r=r, w=w, dma=key)

        dma("sp", ident_f[:], ident_d[:, :], w=["c"], key="c")
        dma("sp", rot_f[:], rotT_d[:, :], w=["c"], key="c")
        dma("sp", gcols[:], gcols_d[:, :], w=["c"], key="c")
        dma("sp", gvec[:], gvec_d[:, :], w=["c"], key="c")
        dma("sp", lamv[:], lamv_d[:, :], w=["c"], key="c")
        S.add("dve", lambda e: e.memset(ones_f[:], 1.0), w=["ones_f"])
        S.add("dve", lambda e: e.memset(eps_c[:], EPS), w=["eps"])
        S.add("dve", lambda e: e.memset(bones_b[:], 0.0), w=["bones"])
        S.add("dve", lambda e: e.tensor_copy(out=ident_b[:], in_=ident_f[:]), r=["c"], w=["k1"])
        S.add("dve", lambda e: e.tensor_copy(out=rotT_b[:], in_=rot_f[:]), r=["c"], w=["k2"])
        S.add("dve", lambda e: e.tensor_copy(out=ones_b[:], in_=ones_f[:]), r=["ones_f"], w=["k3"])
        S.add("dve", lambda e: e.tensor_copy(out=bones_b[0:64, 0:64], in_=ones_f[0:64, 0:64]), r=["ones_f", "bones"], w=["bones"])
        S.add("dve", lambda e: e.tensor_copy(out=bones_b[64:128, 64:128], in_=ones_f[64:128, 64:128]), r=["ones_f", "bones"], w=["bones"])
        S.add("dve", lambda e: e.tensor_scalar(out=gsub08[:], in0=gvec[:, 4:5], scalar1=0.8, scalar2=None, op0=ALU.mult), r=["c"], w=["k5"])
        S.add("dve", lambda e: e.tensor_tensor(out=lamw[:, 0:64], in0=lamv[:, 0:64], in1=lamv[:, 64:128], op=ALU.mult), r=["c"], w=["lw0"])
        S.add("dve", lambda e: e.tensor_tensor(out=lamw[:, 64:128], in0=lamv[:, 128:192], in1=lamv[:, 192:256], op=ALU.mult), r=["c"], w=["lw1"])
        S.add("dve", lambda e: e.reduce_sum(out=lams[:, 0:1], in_=lamw[:, 0:64], axis=AX.X), r=["lw0"], w=["ls0"])
        S.add("dve", lambda e: e.reduce_sum(out=lams[:, 1:2], in_=lamw[:, 64:128], axis=AX.X), r=["lw1"], w=["ls1"])
        S.add("act", lambda e: e.activation(out=lams[:, 2:4], in_=lams[:, 0:2], func=AF.Exp), r=["ls0", "ls1"], w=["ls2"])
        S.add("dve", lambda e: e.tensor_tensor(out=nlam[:], in0=lams[:, 3:4], in1=lams[:, 2:3], op=ALU.subtract), r=["ls2"], w=["nlam"])
        S.add("dve", lambda e: e.tensor_scalar(out=nlam[:], in0=nlam[:], scalar1=-0.2, scalar2=None, op0=ALU.add), r=["nlam"], w=["nlam"])

        def mm(outp, lhsT, rhs, start, stop, r, w):
            S.add("pe", lambda e: e.matmul(outp, lhsT=lhsT, rhs=rhs, start=start, stop=stop), r=r, w=w)

        def norm_phase(stack, src_tile, ntiles, gbase, dst, hkey, tag, use_pool=False, nxt=3):
            xt = [sb(stack, "%s_xt%d" % (tag, i), [128, D], F32) for i in range(nxt)]
            junk = sb(stack, tag + "_junk", [128, D], BF16)
            xh = [sb(stack, "%s_xh%d" % (tag, i), [128, D], BF16) for i in range(2)]
            ssq = sb(stack, tag + "_ssq", [128, ntiles], F32)
            rst = sb(stack, tag + "_rst", [128, ntiles], F32)
            gt = sb(stack, tag + "_gt", [128, 16, 128], F32)
            for c in range(16):
                S.add("dve", lambda e, c=c: e.tensor_scalar(out=gt[:, c, :], in0=ones_f[:], scalar1=gcols[:, gbase + c:gbase + c + 1],
                                                            scalar2=None, op0=ALU.mult), r=["c", "ones_f"], w=[("gt", c)])
            gtk = [("gt", c) for c in range(16)]
            banks = Rot([(0, 1), (2, 3)])
            NX = len(xt)

            def load(i):
                dma("sp", xt[i % NX][:], src_tile(i), w=[("xt", i % NX)], key=("xt", i % NX))

            def square(i):
                bx = i % NX
                S.add("act", lambda e: e.activation(out=junk[:], in_=xt[bx][:], func=AF.Square, accum_out=ssq[:, i:i + 1]),
                      r=[("xt", bx)], w=["junk", ("ssq", i)])

            for i in range(min(NX - 1, ntiles)):
                load(i)
            square(0)
            for i in range(ntiles):
                bx, b2 = i % NX, i % 2
                if i + NX - 1 < ntiles:
                    load(i + NX - 1)
                S.add("act", lambda e, i=i: e.activation(out=rst[:, i:i + 1], in_=ssq[:, i:i + 1], func=AF.Ln, bias=eps_c[:], scale=1.0 / D),
                      r=[("ssq", i), "eps"], w=[("rst", i)])
                if i + 1 < ntiles:
                    square(i + 1)
                S.add("act", lambda e, i=i: e.activation(out=rst[:, i:i + 1], in_=rst[:, i:i + 1], func=AF.Exp, scale=-0.5), r=[("rst", i)], w=[("rst", i)])
                if use_pool:
                    S.add("pool", lambda e, bx=bx, b2=b2, i=i: e.tensor_scalar(out=xh[b2][:], in0=xt[bx][:], scalar1=rst[:, i:i + 1],
                                                                               scalar2=1.0, op0=ALU.mult, op1=ALU.mult),
                          r=[("xt", bx), ("rst", i)], w=[("xh", b2)])
                else:
                    S.add("act", lambda e, bx=bx, b2=b2, i=i: e.activation(out=xh[b2][:], in_=xt[bx][:], func=AF.Copy, scale=rst[:, i:i + 1]),
                          r=[("xt", bx), ("rst", i)], w=[("xh", b2)])
                ba, bb = banks.next()
                for c in range(16):
                    bk = ba if c < 8 else bb
                    S.add("pe", lambda e, c=c, bk=bk, b2=b2: e.transpose(out=PSB[bk][:, (c % 8) * 128:(c % 8 + 1) * 128],
                                                                          in_=xh[b2][:, c * 128:(c + 1) * 128], identity=ident_b[:]),
                          r=[("xh", b2), "k1"], w=[pk(bk)])
                for hf, bk in ((0, ba), (1, bb)):
                    S.add("dve", lambda e, hf=hf, bk=bk, i=i: e.tensor_tensor(out=dst(hf, i), in0=PSB[bk].rearrange("p (c t) -> p c t", c=8),
                                                                             in1=gt[:, hf * 8:(hf + 1) * 8, :], op=ALU.mult),
                          r=[pk(bk)] + gtk, w=[hkey(i)])

        def wload(dst_ap, w_dram, k0, kn, c0, cn, key):
            src = w_dram[k0:k0 + kn, c0:c0 + cn].rearrange("(c p) n -> p c n", p=128)
            dma("pool", dst_ap, src, w=[key], key=key)

        stA0 = contextlib.ExitStack()
        hT_own = sb(stA0, "hT_own", [128, 16, OWN], BF16)
        with contextlib.ExitStack() as stA:
            hT_oth = sb(stA, "hT_oth", [128, 16, S_ALL - OWN], BF16)

            def hdst(hf, i):
                if i < 8:
                    return hT_own[:, hf * 8:(hf + 1) * 8, i * 128:(i + 1) * 128]
                return hT_oth[:, hf * 8:(hf + 1) * 8, (i - 8) * 128:(i - 7) * 128]

            wb = [sb(stA, "wb%d" % i, [128, 16, 256], BF16) for i in range(2)]
            with contextlib.ExitStack() as stA1:
                dma("pool", wb[0][:], w_in[0:D, C_NAQ:C_NAQ + 256].rearrange("(c p) n -> p c n", p=128), w=[("wb", 0)], key=("wb", 0))
                norm_phase(stA1, lambda i: xr[i * 128:(i + 1) * 128, :], 32, 0, hdst, lambda i: ("h", i), "a1", use_pool=True, nxt=4)
                S.emit()

            def hT(c, t0, n):
                if t0 < OWN:
                    return hT_own[:, c, t0:t0 + n]
                return hT_oth[:, c, t0 - OWN:t0 - OWN + n]

            with contextlib.ExitStack() as stA2:
                Ct = sb(stA2, "Ct", [128, S_ALL], F32)
                St = sb(stA2, "St", [128, S_ALL], F32)
                preloaded = [True]
                sq = [sb(stA2, "sq%d" % i, [128, 512], BF16) for i in range(3)]
                sd = [sb(stA2, "sd%d" % i, [128, 512], F32) for i in range(3)]
                Gb = [sb(stA2, "G%d" % i, [128, 512], BF16) for i in range(3)]
                Ab = [sb(stA2, "A%d" % i, [128, 512], BF16) for i in range(3)]
                Bb = [sb(stA2, "B%d" % i, [128, 512], BF16) for i in range(3)]
                yst = [sb(stA2, "yst%d" % i, [128, 512], BF16) for i in range(3)]
                dma("sp", Ct[:], ct_d[:, :], w=["Ct"], key="Ct")
                dma("sp", St[:], st_d[:, :], w=["St"], key="St")
                accb = Rot([0, 1, 2, 3])
                auxb = Rot([4, 5, 6, 7])
                wrot = Rot([0, 1])
                wk = Rot([0, 1, 2])
                yk = Rot([0, 1, 2])

                def post_norm(pbank, nt, gi, dkn, dest, rope, t0):
                    k2 = wk.next()
                    y = yk.next()
                    st = {}

                    def s0():
                        S.add("act", lambda e: e.activation(out=sq[k2][:, :nt], in_=PS[pbank][:, :nt], func=AF.Square), r=[pk(pbank)], w=[("sq", k2)])
                        if rope:
                            S.add("act", lambda e: e.activation(out=Gb[k2][:, :nt], in_=PS[pbank][:, :nt], func=AF.Copy, scale=gvec[:, gi:gi + 1]),
                                  r=[pk(pbank)], w=[("G", k2)])
                        ax = auxb.next()
                        st["ax"] = ax
                        mm(PS[ax][:, :nt], bones_b[:] if rope else ones_b[:], sq[k2][:, :nt], True, True, r=[("sq", k2)], w=[pk(ax)])
                        if rope:
                            S.add("dve", lambda e: e.tensor_tensor(out=Ab[k2][:, :nt], in0=Gb[k2][:, :nt], in1=Ct[:, t0:t0 + nt], op=ALU.mult),
                                  r=[("G", k2), "Ct"], w=[("A", k2)])
                            S.add("dve", lambda e: e.tensor_tensor(out=Bb[k2][:, :nt], in0=Gb[k2][:, :nt], in1=St[:, t0:t0 + nt], op=ALU.mult),
                                  r=[("G", k2), "St"], w=[("B", k2)])

                    def s1():
                        ax = st["ax"]
                        S.add("act", lambda e: e.activation(out=sd[k2][:, :nt], in_=PS[ax][:, :nt], func=AF.Ln, bias=eps_c[:], scale=1.0 / dkn),
                              r=[pk(ax)], w=[("sd", k2)])
                        S.add("act", lambda e: e.activation(out=sd[k2][:, :nt], in_=sd[k2][:, :nt], func=AF.Exp, scale=-0.5), r=[("sd", k2)], w=[("sd", k2)])
                        if not rope:
                            S.add("dve", lambda e: e.scalar_tensor_tensor(out=yst[y][:, :nt], in0=PS[pbank][:, :nt], scalar=gvec[:, gi:gi + 1],
                                                                          in1=sd[k2][:, :nt], op0=ALU.mult, op1=ALU.mult),
                                  r=[pk(pbank), ("sd", k2)], w=[("yst", y)])
                        else:
                            ax2 = auxb.next()
                            mm(PS[ax2][:, :nt], ident_b[:], Ab[k2][:, :nt], True, False, r=[("A", k2)], w=[pk(ax2)])
                            mm(PS[ax2][:, :nt], rotT_b[:], Bb[k2][:, :nt], False, True, r=[("B", k2)], w=[pk(ax2)])
                            S.add("dve", lambda e: e.tensor_tensor(out=yst[y][:, :nt], in0=PS[ax2][:, :nt], in1=sd[k2][:, :nt], op=ALU.mult),
                                  r=[pk(ax2), ("sd", k2)], w=[("yst", y)])
                        dma("sp", dest, yst[y][:, :nt], r=[("yst", y)], key=("yst", y))

                    return [s0, s1]

                pipe = []

                def pipe_step(new_stages=None):
                    for stages in pipe:
                        if stages:
                            stages.pop(0)()
                    while pipe and not pipe[0]:
                        pipe.pop(0)
                    if new_stages is not None:
                        pipe.append(list(new_stages))

                def pipe_drain():
                    while pipe:
                        pipe_step(None)

                def hkeys(t0, n):
                    return [("h", i) for i in range(t0 // 128, (t0 + n + 127) // 128)]

                def fm_proj(colbase, blocks, gi, dkn, destf, rope):
                    for b256 in range(4):
                        wi = wrot.next()
                        if preloaded[0]:
                            preloaded[0] = False
                        else:
                            wload(wb[wi][:], w_in, 0, D, colbase + b256 * 256, 256, ("wb", wi))
                        for half in range(2):
                            n = b256 * 2 + half
                            for (t0, nt, d0) in blocks:
                                bank = accb.next()
                                for c in range(KC):
                                    mm(PS[bank][:, :nt], wb[wi][:, c, half * 128:(half + 1) * 128], hT(c, t0, nt), c == 0, c == KC - 1,
                                       r=[("wb", wi)] + hkeys(t0, nt), w=[pk(bank)])
                                pipe_step(post_norm(bank, nt, gi, dkn, destf(n, d0, nt), rope, t0))

                vst = [sb(stA2, "vst%d" % i, [128, 512], BF16) for i in range(2)]
                vk = Rot([0, 1])

                def tm_proj(colbase, tiles, destf):
                    for b256 in range(4):
                        wi = wrot.next()
                        wload(wb[wi][:], w_in, 0, D, colbase + b256 * 256, 256, ("wb", wi))
                        for p0 in range(0, len(tiles), 2):
                            bank = accb.next()
                            for tt in range(2):
                                t0 = tiles[p0 + tt]
                                for c in range(KC):
                                    mm(PS[bank][:, tt * 256:(tt + 1) * 256], hT(c, t0, 128), wb[wi][:, c, :], c == 0, c == KC - 1,
                                       r=[("wb", wi)] + hkeys(t0, 128), w=[pk(bank)])
                            v = vk.next()
                            S.add("act", lambda e, bank=bank, v=v: e.copy(out=vst[v][:], in_=PS[bank]), r=[pk(bank)], w=[("vst", v)])
                            for tt in range(2):
                                dma("sp", destf(p0 + tt, b256), vst[v][:, tt * 256:(tt + 1) * 256].rearrange("p (h d) -> p h d", h=2),
                                    r=[("vst", v)], key=("vst", v))

                na_blocks = [(S_ALL - 256, 256, 0), (0, 512, 256), (512, 512, 768), (1024, 256, 1280)]
                own_blocks = [(0, 512, 0), (512, 512, 512)]
                all_blocks = [(t * 512, 512, t * 512) for t in range(8)]
                fm_proj(C_NAQ, own_blocks, 0, 128, lambda n, d0, nt: QNA[n, :, d0:d0 + nt], False)
                fm_proj(C_NAK, na_blocks, 1, 128, lambda n, d0, nt: KNA[n, :, d0:d0 + nt], False)
                na_tiles = [S_ALL - 256, S_ALL - 128] + [i * 128 for i in range(10)]
                pipe_drain()
                tm_proj(C_NAV, na_tiles, lambda m, b: VNA[2 * b:2 * b + 2, m * 128:(m + 1) * 128, :].rearrange("h p d -> p h d"))
                fm_proj(C_DFQ, own_blocks, 2, 64, lambda n, d0, nt: QDF[n, :, d0:d0 + nt], True)
                fm_proj(C_DFK, all_blocks, 3, 64, lambda n, d0, nt: KDF[n, :, d0:d0 + nt], True)
                pipe_drain()
                tm_proj(C_DFV, [i * 128 for i in range(32)], lambda m, b: VDF[2 * b:2 * b + 2, :, m, :].rearrange("h p d -> p h d"))
                S.emit()

        stBC = contextlib.ExitStack()
        oaT = sb(stBC, "oaT", [128, 8, OWN], BF16)
        obT = sb(stBC, "obT", [128, 8, OWN], BF16)

        with contextlib.ExitStack() as stB:
            T2 = sb(stB, "T2", [128, 8, 15, 64], F32)
            rmk = sb(stB, "rmk", [128, 16, 6, 64], F32)
            kT = [sb(stB, "nkT%d" % i, [128, 1536], BF16) for i in range(2)]
            qT = [sb(stB, "nqT%d" % i, [128, OWN], BF16) for i in range(2)]
            Ve = [sb(stB, "nVe%d" % i, [128, 12, 128], BF16) for i in range(2)]
            Vo = [sb(stB, "nVo%d" % i, [128, 11, 128], BF16) for i in range(2)]
            sbf = [sb(stB, "nsb%d" % i, [128, 6, 64], F32) for i in range(4)]
            Pn = [sb(stB, "nP%d" % i, [128, 6, 64], BF16) for i in range(4)]
            rcn = [sb(stB, "nrc%d" % i, [128, 512], F32) for i in range(2)]
            dma("sp", T2[:, 0, :, :].rearrange("p d c -> p (d c)"), t2_d[:, 0:960], w=["T2a"], key="T2a")
            dma("sp", rmk[:, 0:4, :, :].rearrange("p r c q -> p (r c q)"), rm_d[:, 0:4 * 384], w=["rmka"], key="rmka")
            dma("sp", T2[:, 1:8, :, :].rearrange("p h d c -> p (h d c)"), t2_d[:, 960:], w=["T2b"], key="T2b")
            dma("sp", rmk[:, 4:16, :, :].rearrange("p r c q -> p (r c q)"), rm_d[:, 4 * 384:], w=["rmkb"], key="rmkb")
            sbank = Rot([0, 1, 2, 3])
            obank = Rot([(4, 5), (6, 7)])
            sk = Rot([0, 1, 2, 3])
            na_scale = 128.0 ** -0.5
            pipe = []

            def na_unit(h, hb, half, rr, bo, bz):
                rl = half * 8 + rr
                if rl <= 3:
                    vlo, vhi = rl, 11
                elif rl <= 12:
                    vlo, vhi = rl, rl + 7
                else:
                    vlo, vhi = 12, rl + 7
                v0s = list(range(vlo, vhi + 1, 2))
                n = len(v0s)
                drp0 = vlo - rl + 3
                bs = sbank.next()
                k2 = sk.next()
                for ci, v0 in enumerate(v0s):
                    mm(PS[bs][:, ci * 64:(ci + 1) * 64], kT[hb][:, v0 * 64:v0 * 64 + 128], qT[hb][:, rl * 64:(rl + 1) * 64], True, True,
                       r=[("kT", hb), ("qT", hb)], w=[pk(bs)])
                S.add("dve", lambda e: e.scalar_tensor_tensor(
                    out=sbf[k2][:, 0:n, :], in0=PS[bs][:, 0:n * 64].rearrange("p (n c) -> p n c", n=n), scalar=na_scale,
                    in1=T2[:, h, drp0:drp0 + 2 * n - 1:2, :], op0=ALU.mult, op1=ALU.add),
                    r=[pk(bs), "T2a" if h == 0 else "T2b"], w=[("sbf", k2)])
                if not (4 <= rl <= 12):
                    S.add("dve", lambda e: e.tensor_tensor(out=sbf[k2][:, 0:n, :], in0=sbf[k2][:, 0:n, :], in1=rmk[:, rl, 0:n, :], op=ALU.add),
                          r=[("sbf", k2), "rmka" if rl < 4 else "rmkb"], w=[("sbf", k2)])
                S.add("act", lambda e: e.activation(out=Pn[k2][:, 0:n, :], in_=sbf[k2][:, 0:n, :], func=AF.Exp),
                      r=[("sbf", k2)], w=[("Pn", k2)])

                def back():
                    for ci, v0 in enumerate(v0s):
                        vt = Ve[hb][:, v0 // 2, :] if v0 % 2 == 0 else Vo[hb][:, (v0 - 1) // 2, :]
                        mm(PS[bo][:, rr * 64:(rr + 1) * 64], vt, Pn[k2][:, ci, :], ci == 0, ci == n - 1,
                           r=[("Pn", k2), ("Ve", hb), ("Vo", hb)], w=[pk(bo)])
                    for ci in range(n):
                        mm(PS[bz][:, rr * 64:(rr + 1) * 64], ones_b[:], Pn[k2][:, ci, :], ci == 0, ci == n - 1,
                           r=[("Pn", k2)], w=[pk(bz)])
                    if rr == 7:
                        r2 = (h * 2 + half) % 2

                        def fin():
                            S.add("act", lambda e: e.activation(out=rcn[r2][:], in_=PS[bz], func=AF.Ln), r=[pk(bz)], w=[("rcn", r2)])
                            S.add("act", lambda e: e.activation(out=rcn[r2][:], in_=rcn[r2][:], func=AF.Exp, scale=-1.0), r=[("rcn", r2)], w=[("rcn", r2)])
                            S.add("dve", lambda e: e.tensor_tensor(out=oaT[:, h, half * 512:(half + 1) * 512], in0=PS[bo],
                                                                   in1=rcn[r2][:], op=ALU.mult),
                                  r=[pk(bo), ("rcn", r2)], w=[("oaT", h, half)])
                        fins.append([4, fin])
                return back

            LAG = 3
            fins = []
            for h in range(8):
                hb = h % 2
                dma("sp", kT[hb][:], KNA[h, :, :], w=[("kT", hb)], key=("kT", hb))
                dma("sp", qT[hb][:], QNA[h, :, :], w=[("qT", hb)], key=("qT", hb))
                dma("sp", Ve[hb][:], VNA[h, :, :].rearrange("(m p) d -> p m d", p=128), w=[("Ve", hb)], key=("Ve", hb))
                dma("sp", Vo[hb][:], VNA[h, 64:64 + 11 * 128, :].rearrange("(m p) d -> p m d", p=128), w=[("Vo", hb)], key=("Vo", hb))
                for half in range(2):
                    bo, bz = obank.next()
                    for rr in range(8):
                        pipe.append(na_unit(h, hb, half, rr, bo, bz))
                        if len(pipe) > LAG:
                            pipe.pop(0)()
                        for f in fins:
                            f[0] -= 1
                        while fins and fins[0][0] <= 0:
                            fins.pop(0)[1]()
            while pipe:
                pipe.pop(0)()
            while fins:
                fins.pop(0)[1]()
            if DEBUG:
                dma("sp", OAT[:, :, :], oaT[:], r=[("oaT", h, hf) for h in range(8) for hf in range(2)], key="dbg")
            S.emit()

        with contextlib.ExitStack() as stC:
            kT = [sb(stC, "dkT%d" % i, [128, S_ALL], BF16) for i in range(2)]
            qz0 = [sb(stC, "dqz0%d" % i, [128, OWN], BF16) for i in range(2)]
            qz1 = [sb(stC, "dqz1%d" % i, [128, OWN], BF16) for i in range(2)]
            for i in range(2):
                S.add("dve", lambda e, i=i: e.memset(qz0[i][64:128, :], 0.0), w=[("qz0z", i)])
                S.add("dve", lambda e, i=i: e.memset(qz1[i][0:64, :], 0.0), w=[("qz1z", i)])
            Vd = [sb(stC, "dV%d" % i, [128, 32, 128], BF16) for i in range(2)]
            Pd = [sb(stC, "dP%d" % i, [128, 1024], BF16) for i in range(3)]
            r0 = sb(stC, "dr0", [128, 512], F32)
            r1 = sb(stC, "dr1", [128, 512], F32)
            t0b = sb(stC, "dt0", [128, 512], F32)
            a0s = sb(stC, "da0s", [128, 512], BF16)
            t1b = sb(stC, "dt1", [128, 512], F32)
            dd = sb(stC, "ddd", [128, 512], F32)
            sqd = sb(stC, "dsq", [128, 512], BF16)
            sdd = sb(stC, "dsd", [128, 512], F32)
            df_scale = 64.0 ** -0.5
            sbank = Rot([(0, 1), (2, 3)])
            pkk = Rot([0, 1, 2])
            BO0, BO1, BZ0, BZ1 = 4, 5, 6, 7
            for h in range(8):
                hb = h % 2
                dma("sp", kT[hb][:], KDF[h, :, :], w=[("kT", hb)], key=("kT", hb))
                dma("sp", qz0[hb][0:64, :], QDF[h, 0:64, :], w=[("qz0", hb)], key=("qz0", hb))
                dma("sp", qz1[hb][64:128, :], QDF[h, 64:128, :], w=[("qz1", hb)], key=("qz1", hb))
                dma("sp", Vd[hb][:], VDF[h, :, :, :], w=[("Vd", hb)], key=("Vd", hb))
                for qb in range(2):
                    qs = slice(qb * 512, (qb + 1) * 512)

                    def qk(kc):
                        b0, b1 = sbank.next()
                        mm(PS[b0], kT[hb][:, kc * 128:(kc + 1) * 128], qz0[hb][:, qs], True, True,
                           r=[("kT", hb), ("qz0", hb), ("qz0z", hb)], w=[pk(b0)])
                        mm(PS[b1], kT[hb][:, kc * 128:(kc + 1) * 128], qz1[hb][:, qs], True, True,
                           r=[("kT", hb), ("qz1", hb), ("qz1z", hb)], w=[pk(b1)])
                        p = pkk.next()
                        S.add("act", lambda e, b0=b0, p=p: e.activation(out=Pd[p][:], in_=ps_all[:, b0 * 512:(b0 + 2) * 512], func=AF.Exp, scale=df_scale),
                              r=[pk(b0), pk(b1)], w=[("Pd", p)])
                        return p

                    def av(kc, p):
                        st_, sp_ = kc == 0, kc == 31
                        mm(PS[BO0], Vd[hb][:, kc, :], Pd[p][:, 0:512], st_, sp_, r=[("Pd", p), ("Vd", hb)], w=[pk(BO0)])
                        mm(PS[BO1], Vd[hb][:, kc, :], Pd[p][:, 512:1024], st_, sp_, r=[("Pd", p), ("Vd", hb)], w=[pk(BO1)])
                        mm(PS[BZ1], ones_b[:], Pd[p][:, 512:1024], st_, sp_, r=[("Pd", p)], w=[pk(BZ1)])
                        if st_:
                            S.add("dve", lambda e, p=p: e.tensor_copy(out=PS[BZ0], in_=Pd[p][:, 0:512]), r=[("Pd", p)], w=[pk(BZ0)])
                        else:
                            S.add("dve", lambda e, p=p: e.tensor_tensor(out=PS[BZ0], in0=PS[BZ0], in1=Pd[p][:, 0:512], op=ALU.add),
                                  r=[("Pd", p), pk(BZ0)], w=[pk(BZ0)])

                    pq = [qk(0), qk(1)]
                    for kc in range(32):
                        if kc + 2 < 32:
                            pq.append(qk(kc + 2))
                        av(kc, pq.pop(0))
                    S.add("dve", lambda e: e.tensor_copy(out=a0s[:], in_=PS[BZ0]), r=[pk(BZ0)], w=["a0s"])
                    mm(PS[BZ0], ones_b[:], a0s[:], True, True, r=["a0s"], w=[pk(BZ0)])
                    S.add("act", lambda e: e.activation(out=r1[:], in_=PS[BZ1], func=AF.Ln), r=[pk(BZ1)], w=["r1"])
                    S.add("act", lambda e: e.activation(out=r1[:], in_=r1[:], func=AF.Exp, scale=-1.0), r=["r1"], w=["r1"])
                    S.add("act", lambda e: e.activation(out=r0[:], in_=PS[BZ0], func=AF.Ln), r=[pk(BZ0)], w=["r0"])
                    S.add("act", lambda e: e.activation(out=r0[:], in_=r0[:], func=AF.Exp, scale=-1.0), r=["r0"], w=["r0"])
                    S.add("dve", lambda e: e.tensor_tensor(out=t1b[:], in0=PS[BO1], in1=r1[:], op=ALU.mult), r=[pk(BO1), "r1"], w=["t1"])
                    S.add("dve", lambda e: e.tensor_tensor(out=t0b[:], in0=PS[BO0], in1=r0[:], op=ALU.mult), r=[pk(BO0), "r0"], w=["t0"])
                    S.add("dve", lambda e: e.scalar_tensor_tensor(out=dd[:], in0=t1b[:], scalar=nlam[:, 0:1], in1=t0b[:], op0=ALU.mult, op1=ALU.add),
                          r=["t0", "t1"], w=["dd"])
                    S.add("pool", lambda e: e.tensor_tensor(out=sqd[:], in0=dd[:], in1=dd[:], op=ALU.mult), r=["dd"], w=["sqd"])
                    mm(PS[BZ1], ones_b[:], sqd[:], True, True, r=["sqd"], w=[pk(BZ1)])
                    S.add("act", lambda e: e.activation(out=sdd[:], in_=PS[BZ1], func=AF.Ln, bias=eps_c[:], scale=1.0 / 128), r=[pk(BZ1)], w=["sdd"])
                    S.add("act", lambda e: e.activation(out=sdd[:], in_=sdd[:], func=AF.Exp, scale=-0.5), r=["sdd"], w=["sdd"])
                    S.add("dve", lambda e, h=h, qs=qs: e.scalar_tensor_tensor(out=obT[:, h, qs], in0=dd[:], scalar=gsub08[:, 0:1], in1=sdd[:],
                                                                           op0=ALU.mult, op1=ALU.mult),
                          r=["dd", "sdd"], w=[("obT", h, qb)])
            if DEBUG:
                dma("sp", OBT[:, :, :], obT[:], r=[("obT", h, hf) for h in range(8) for hf in range(2)], key="dbg")
            S.emit()

        def tm_residual(stack, tag, srcs, res_dram, dst_dram, gate=None, akeys=None, mid=None):
            wbufs = []
            for si, (aT, nk, wd) in enumerate(srcs):
                wbufs.append([sb(stack, "%s_w%d_%d" % (tag, si, i), [128, nk, 512], BF16) for i in range(2)])
            res = [sb(stack, "%s_res%d" % (tag, i), [128, 512], F32) for i in range(3)]
            ost = [sb(stack, "%s_ost%d" % (tag, i), [128, 512], F32) for i in range(3)]
            sg = [sb(stack, "%s_sg%d" % (tag, i), [128, 512], F32) for i in range(2)] if gate else None
            bankr = Rot([(0, 1), (2, 3), (4, 5), (6, 7)]) if gate else Rot([(i,) for i in range(8)])
            def res_load(u):
                nb_, tt_ = divmod(u, 8)
                k3_ = u % 3
                dma("sp", res[k3_][:], res_dram[tt_ * 128:(tt_ + 1) * 128, nb_ * 512:(nb_ + 1) * 512], w=[(tag + "res", k3_)], key=(tag + "res", k3_))

            for nb_ in range(2):
                for si, (aT, nk, wd) in enumerate(srcs):
                    wload(wbufs[si][nb_][:], wd, 0, nk * 128, nb_ * 512, 512, (tag + "w%d" % si, nb_))
            if mid is not None:
                mid()
            res_load(0)
            res_load(1)
            for nb in range(4):
                wi = nb % 2
                cs = slice(nb * 512, (nb + 1) * 512)
                if nb > 1:
                    for si, (aT, nk, wd) in enumerate(srcs):
                        wload(wbufs[si][wi][:], wd, 0, nk * 128, nb * 512, 512, (tag + "w%d" % si, wi))
                for tt in range(8):
                    tsl = slice(tt * 128, (tt + 1) * 128)
                    banks = bankr.next()
                    u = nb * 8 + tt
                    k3 = u % 3
                    if u + 2 < 32:
                        res_load(u + 2)
                    for si, (aT, nk, wd) in enumerate(srcs):
                        ak = list(akeys[si](tt)) if akeys else []
                        for c in range(nk):
                            mm(PS[banks[si]], aT[:, c, tsl], wbufs[si][wi][:, c, :], c == 0, c == nk - 1, r=[(tag + "w%d" % si, wi)] + ak, w=[pk(banks[si])])
                    if gate:
                        g2 = k3 % 2
                        S.add("act", lambda e, b=banks[0], g2=g2: e.activation(out=sg[g2][:], in_=PS[b], func=AF.Sigmoid), r=[pk(banks[0])], w=[(tag + "sg", g2)])
                        S.add("dve", lambda e, b=banks[1], g2=g2: e.tensor_tensor(out=sg[g2][:], in0=PS[b], in1=sg[g2][:], op=ALU.mult),
                              r=[pk(banks[1]), (tag + "sg", g2)], w=[(tag + "sg", g2)])
                        S.add("dve", lambda e, g2=g2, k3=k3: e.tensor_tensor(out=ost[k3][:], in0=sg[g2][:], in1=res[k3][:], op=ALU.add),
                              r=[(tag + "sg", g2), (tag + "res", k3)], w=[(tag + "ost", k3)])
                    else:
                        S.add("dve", lambda e, b=banks[0], k3=k3: e.tensor_tensor(out=ost[k3][:], in0=PS[b], in1=res[k3][:], op=ALU.add),
                              r=[pk(banks[0]), (tag + "res", k3)], w=[(tag + "ost", k3)])
                    dma("sp", dst_dram[tsl, cs], ost[k3][:], r=[(tag + "ost", k3)], key=(tag + "ost", k3))

        stD = contextlib.ExitStack()
        mT = sb(stD, "mT", [128, 16, OWN], BF16)
        with contextlib.ExitStack() as stD1:
            wna = [sb(stD1, "wna%d" % i, [128, 8, 128], BF16) for i in range(2)]
            wdf = [sb(stD1, "wdf%d" % i, [128, 8, 128], BF16) for i in range(2)]
            wga = [sb(stD1, "wga%d" % i, [128, 16, 128], BF16) for i in range(2)]
            wgb = [sb(stD1, "wgb%d" % i, [128, 16, 128], BF16) for i in range(2)]
            sga = [sb(stD1, "sga%d" % i, [128, 512], F32) for i in range(2)]
            sgb = [sb(stD1, "sgb%d" % i, [128, 512], F32) for i in range(2)]
            ta = [sb(stD1, "ta%d" % i, [128, 512], F32) for i in range(2)]
            tb = [sb(stD1, "tb%d" % i, [128, 512], F32) for i in range(2)]
            bankr = Rot([(0, 1, 2, 3), (4, 5, 6, 7)])
            kk = Rot([0, 1])
            for b128 in range(16):
                wi = b128 % 2
                c0 = b128 * 128
                wload(wna[wi][:], w_na_out, 0, 1024, c0, 128, ("wna", wi))
                wload(wdf[wi][:], w_df_out, 0, 1024, c0, 128, ("wdf", wi))
                wload(wga[wi][:], w_in, 0, D, C_GA + c0, 128, ("wga", wi))
                wload(wgb[wi][:], w_in, 0, D, C_GB + c0, 128, ("wgb", wi))
                for half in range(1):
                    n = b128
                    hs = slice(0, 128)
                    for tb_ in range(2):
                        ts_ = slice(tb_ * 512, (tb_ + 1) * 512)
                        bya, byb, bga, bgb = bankr.next()
                        for c in range(8):
                            mm(PS[bya], wna[wi][:, c, hs], oaT[:, c, ts_], c == 0, c == 7, r=[("wna", wi)], w=[pk(bya)])
                        for c in range(8):
                            mm(PS[byb], wdf[wi][:, c, hs], obT[:, c, ts_], c == 0, c == 7, r=[("wdf", wi)], w=[pk(byb)])
                        for c in range(16):
                            mm(PS[bga], wga[wi][:, c, hs], hT_own[:, c, ts_], c == 0, c == 15, r=[("wga", wi)], w=[pk(bga)])
                        for c in range(16):
                            mm(PS[bgb], wgb[wi][:, c, hs], hT_own[:, c, ts_], c == 0, c == 15, r=[("wgb", wi)], w=[pk(bgb)])
                        k2 = kk.next()
                        S.add("act", lambda e, bga=bga, k2=k2: e.activation(out=sga[k2][:], in_=PS[bga], func=AF.Sigmoid), r=[pk(bga)], w=[("sga", k2)])
                        S.add("act", lambda e, bgb=bgb, k2=k2: e.activation(out=sgb[k2][:], in_=PS[bgb], func=AF.Sigmoid), r=[pk(bgb)], w=[("sgb", k2)])
                        S.add("dve", lambda e, bya=bya, k2=k2: e.tensor_tensor(out=ta[k2][:], in0=PS[bya], in1=sga[k2][:], op=ALU.mult),
                              r=[pk(bya), ("sga", k2)], w=[("ta", k2)])
                        S.add("dve", lambda e, byb=byb, k2=k2: e.tensor_tensor(out=tb[k2][:], in0=PS[byb], in1=sgb[k2][:], op=ALU.mult),
                              r=[pk(byb), ("sgb", k2)], w=[("tb", k2)])
                        S.add("dve", lambda e, k2=k2, n=n, ts_=ts_: e.tensor_tensor(out=mT[:, n, ts_], in0=ta[k2][:], in1=tb[k2][:], op=ALU.add),
                              r=[("ta", k2), ("tb", k2)], w=[("mT", n, ts_.start)])
            if DEBUG:
                dma("sp", MT[:, :, :], mT[:], r=[("mT", n, t) for n in range(16) for t in (0, 512)], key="dbg")
            tm_residual(stD1, "d2", [(mT, 16, w_o)], xr[0:OWN, :], X1, akeys=[lambda tt: [("mT", n, (tt // 4) * 512) for n in range(16)]])
            S.emit()

        stD.close()
        stBC.close()
        stA0.close()

        with contextlib.ExitStack() as stE:
            actT = sb(stE, "actT", [128, HC, OWN], BF16)
            with contextlib.ExitStack() as stE12:
                hfT = sb(stE12, "hfT", [128, 16, OWN], BF16)
                with contextlib.ExitStack() as stE1:
                    stE2 = stE1
                    wg = [sb(stE2, "wg%d" % i, [128, 16, 256], BF16) for i in range(2)]
                    wu = [sb(stE2, "wu%d" % i, [128, 16, 256], BF16) for i in range(2)]
                    for pb_ in range(2):
                        wload(wg[pb_][:], w_gate, 0, D, pb_ * 256, 256, ("wg", pb_))
                        wload(wu[pb_][:], w_up, 0, D, pb_ * 256, 256, ("wu", pb_))
                    norm_phase(stE1, lambda i: X1[i * 128:(i + 1) * 128, :], 8, 16,
                               lambda hf, i: hfT[:, hf * 8:(hf + 1) * 8, i * 128:(i + 1) * 128], lambda i: ("h", i), "e1", use_pool=True)
                    sl = [sb(stE2, "sl%d" % i, [128, 512], F32) for i in range(2)]
                    bankr = Rot([(0, 1), (2, 3), (4, 5), (6, 7)])
                    kk = Rot([0, 1])
                    for b256 in range(22):
                        wi = b256 % 2
                        if b256 > 1:
                            wload(wg[wi][:], w_gate, 0, D, b256 * 256, 256, ("wg", wi))
                            wload(wu[wi][:], w_up, 0, D, b256 * 256, 256, ("wu", wi))
                        for half in range(2):
                            n = b256 * 2 + half
                            hs = slice(half * 128, (half + 1) * 128)
                            for tb_ in range(2):
                                ts_ = slice(tb_ * 512, (tb_ + 1) * 512)
                                bg, bu = bankr.next()
                                hk_ = [("h", i) for i in range(tb_ * 4, tb_ * 4 + 4)]
                                for c in range(16):
                                    mm(PS[bg], wg[wi][:, c, hs], hfT[:, c, ts_], c == 0, c == 15, r=[("wg", wi)] + hk_, w=[pk(bg)])
                                for c in range(16):
                                    mm(PS[bu], wu[wi][:, c, hs], hfT[:, c, ts_], c == 0, c == 15, r=[("wu", wi)] + hk_, w=[pk(bu)])
                                k2 = kk.next()
                                S.add("act", lambda e, bg=bg, k2=k2: e.activation(out=sl[k2][:], in_=PS[bg], func=AF.Silu), r=[pk(bg)], w=[("sl", k2)])
                                S.add("dve", lambda e, bu=bu, k2=k2, n=n, ts_=ts_: e.tensor_tensor(out=actT[:, n, ts_], in0=PS[bu], in1=sl[k2][:], op=ALU.mult),
                                      r=[pk(bu), ("sl", k2)], w=[("actT", n, ts_.start)])
                    S.emit()
            with contextlib.ExitStack() as stE3:
                wd_ = [sb(stE3, "wd%d" % i, [128, 4, 512], BF16) for i in range(3)]
                res = [sb(stE3, "e3res%d" % i, [128, 512], F32) for i in range(8)]
                ost = [sb(stE3, "e3ost%d" % i, [128, 512], F32) for i in range(3)]
                wr = Rot([0, 1, 2])
                rk = Rot([0, 1, 2])
                for nb in range(4):
                    cs = slice(nb * 512, (nb + 1) * 512)
                    for tt in range(8):
                        dma("sp", res[tt][:], X1[tt * 128:(tt + 1) * 128, cs], w=[("e3res", tt)], key=("e3res", tt))
                    for kg in range(11):
                        wi = wr.next()
                        wload(wd_[wi][:], w_down, kg * 512, 512, nb * 512, 512, ("wd", wi))
                        if kg in (0, 10):
                            order = [(kq, tt) for tt in range(8) for kq in range(4)]
                        else:
                            order = [(kq, tt) for kq in range(4) for tt in range(8)]
                        for kq, tt in order:
                            kc = kg * 4 + kq
                            mm(PS[tt], actT[:, kc, tt * 128:(tt + 1) * 128], wd_[wi][:, kq, :], kc == 0, kc == HC - 1, r=[("wd", wi)], w=[pk(tt)])
                    for tt in range(8):
                        tsl = slice(tt * 128, (tt + 1) * 128)
                        k3 = rk.next()
                        S.add("dve", lambda e, tt=tt, k3=k3: e.tensor_tensor(out=ost[k3][:], in0=PS[tt], in1=res[tt][:], op=ALU.add),
                              r=[pk(tt), ("e3res", tt)], w=[("e3ost", k3)])
                        dma("sp", X2[tsl, cs], ost[k3][:], r=[("e3ost", k3)], key=("e3ost", k3))
                S.emit()

        with contextlib.ExitStack() as stF:
            hpT = sb(stF, "hpT", [128, 16, OWN], BF16)
            pT = sb(stF, "pT", [128, 2, OWN], BF16)
            with contextlib.ExitStack() as stF1:
                pt = [sb(stF1, "pt%d" % i, [128, 256], F32) for i in range(2)]
                pb = [sb(stF1, "pb%d" % i, [128, 256], BF16) for i in range(2)]

                def f_mid():
                    norm_phase(stF1, lambda i: X2[i * 128:(i + 1) * 128, :], 8, 32,
                               lambda hf, i: hpT[:, hf * 8:(hf + 1) * 8, i * 128:(i + 1) * 128], lambda i: ("h", i), "f1", use_pool=True)
                    for i in range(8):
                        b2 = i % 2
                        dma("sp", pt[b2][:], pr[i * 128:(i + 1) * 128, :], w=[("pt", b2)], key=("pt", b2))
                        S.add("act", lambda e, b2=b2: e.copy(out=pb[b2][:], in_=pt[b2][:]), r=[("pt", b2)], w=[("pb", b2)])
                        bk = 4 + b2
                        for c in range(2):
                            S.add("pe", lambda e, c=c, bk=bk, b2=b2: e.transpose(out=PSB[bk][:, c * 128:(c + 1) * 128], in_=pb[b2][:, c * 128:(c + 1) * 128],
                                                                                  identity=ident_b[:]), r=[("pb", b2)], w=[pk(bk)])
                        S.add("dve", lambda e, bk=bk, i=i: e.tensor_copy(out=pT[:, :, i * 128:(i + 1) * 128],
                                                                         in_=PSB[bk][:, 0:256].rearrange("p (c t) -> p c t", c=2)),
                              r=[pk(bk)], w=[("pT", i)])

                tm_residual(stF1, "f2", [(hpT, 16, w_pg), (pT, 2, w_pp)], X2, out, gate=True,
                            akeys=[lambda tt: [("h", tt)], lambda tt: [("pT", tt)]], mid=f_mid)
                S.emit()
        print("bass program: ops=%d waits=%d dma_sems=%d" % (S.nops, S.nwaits, len(S.dsem)))
    return nc


_NC_CACHE = {}


def _host_consts():
    ident = np.eye(128, dtype=np.float32)
    R = np.zeros((128, 128), np.float32)
    for p in range(128):
        if p % 64 < 32:
            R[p, p + 32] = -1.0
        else:
            R[p, p - 32] = 1.0
    rotT = np.ascontiguousarray(R.T)
    inv = (1.0 / (10000.0 ** (np.arange(0, 64, 2, dtype=np.float32) / np.float32(64)))).astype(np.float32)
    ang = np.arange(S_ALL, dtype=np.float32)[:, None] * inv[None, :]
    cos = np.cos(ang).astype(np.float32)
    sin = np.sin(ang).astype(np.float32)
    return ident, rotT, cos, sin


def _t2_table(rpb):
    c = np.arange(64)
    cs = np.clip(c - 8, 0, 48)
    kc = np.arange(64)
    inwin = (kc[:, None] >= cs[None, :]) & (kc[:, None] < cs[None, :] + 16)
    dc = np.clip(kc[:, None] - c[None, :] + 15, 0, 30)
    t2 = np.full((2, 64, 8, 15, 64), NEG, np.float32)
    for a in range(2):
        for drp in range(15):
            dr = drp + a
            if dr > 14:
                continue
            vals = rpb[:, dr, :][:, dc]
            vals = np.where(inwin[None], vals, np.float32(NEG))
            t2[a, :, :, drp, :] = vals.transpose(1, 0, 2)
    return np.ascontiguousarray(t2.reshape(128, 8 * 15 * 64))


def _rowmask(j):
    rm = np.zeros((2, 64, 16, 6), np.float32)
    for rl in range(16):
        r = 16 * j + rl
        rs = min(max(r - 4, 0), 56)
        if rl <= 3:
            vlo, vhi = rl, 11
        elif rl <= 12:
            vlo, vhi = rl, rl + 7
        else:
            vlo, vhi = 12, rl + 7
        for ci, v0 in enumerate(range(vlo, vhi + 1, 2)):
            for a in range(2):
                rho = 16 * j - 4 + v0 + a
                ok = rs <= rho <= rs + 7
                rm[a, :, rl, ci] = 0.0 if ok else NEG
    return np.ascontiguousarray(np.broadcast_to(rm.reshape(128, 96, 1), (128, 96, 64)).reshape(128, 96 * 64))


def kernel(x, p, g_mix, w_in, g_na_q, g_na_k, na_rpb, g_df_q, g_df_k, lam_q1, lam_k1, lam_q2, lam_k2,
           g_df_sub, w_na_out, w_df_out, w_o, g_ffn, w_gate, w_up, w_down, g_ple, w_ple_gate, w_ple_proj):
    f = lambda a: np.ascontiguousarray(np.asarray(a, dtype=np.float32))
    x = f(x); p = f(p)
    if "nc" not in _NC_CACHE:
        _NC_CACHE["nc"] = build_program()
    nc = _NC_CACHE["nc"]
    ident, rotT, cos, sin = _host_consts()
    col = lambda g: f(g).reshape(16, 128).T
    gcols = np.ascontiguousarray(np.concatenate([col(g_mix[0]), col(g_ffn[0]), col(g_ple[0])], axis=1))
    gvec = np.zeros((128, 8), np.float32)
    gvec[:, 0] = f(g_na_q[0]); gvec[:, 1] = f(g_na_k[0])
    gvec[:, 2] = np.tile(f(g_df_q[0]), 2); gvec[:, 3] = np.tile(f(g_df_k[0]), 2)
    gvec[:, 4] = f(g_df_sub[0])
    lamv = np.ascontiguousarray(np.broadcast_to(np.concatenate([f(lam_q1[0]), f(lam_k1[0]), f(lam_q2[0]), f(lam_k2[0])])[None, :], (128, 256)))
    t2 = _t2_table(f(na_rpb[0]))
    shared = {
        "w_in": f(w_in[0]), "w_na_out": f(w_na_out[0]), "w_df_out": f(w_df_out[0]), "w_o": f(w_o[0]),
        "w_gate": f(w_gate[0]), "w_up": f(w_up[0]), "w_down": f(w_down[0]), "w_ple_gate": f(w_ple_gate[0]),
        "w_ple_proj": f(w_ple_proj[0]), "gcols": gcols, "gvec": gvec, "lamv": lamv, "ident": ident, "rotT": rotT, "t2": t2,
    }
    in_maps = []
    for core in range(8):
        b, j = core // 4, core % 4
        pos = (np.arange(S_ALL) + OWN * j) % S_ALL
        m = dict(shared)
        m["xr"] = np.ascontiguousarray(x[b][pos])
        m["pr"] = np.ascontiguousarray(p[0, b, OWN * j:OWN * (j + 1)])
        m["ropec"] = np.ascontiguousarray(np.tile(cos[pos].T, (4, 1)))
        m["ropes"] = np.ascontiguousarray(np.tile(sin[pos].T, (4, 1)))
        m["rowmask"] = _rowmask(j)
        in_maps.append(m)
    res = run_bass_kernel_spmd(nc, in_maps, core_ids=list(range(8)))
    _NC_CACHE["last"] = res
    outp = np.empty((2, S_ALL, D), np.float32)
    for core in range(8):
        b, j = core // 4, core % 4
        outp[b, OWN * j:OWN * (j + 1)] = res.results[core]["out"]
    return outp
```

```python
import contextlib
import math
import numpy as np
import concourse.bass as bass
import concourse.mybir as mybir
from concourse.bass_utils import run_bass_kernel_spmd

F32 = mybir.dt.float32
BF16 = mybir.dt.bfloat16
AF = mybir.ActivationFunctionType
ALU = mybir.AluOpType
AX = mybir.AxisListType

ENG = ["pe", "act", "dve", "pool", "sp"]

D = 2048
KC = 16
S_ALL = 4096
OWN = 1024
HID = 5632
HC = 44
EPS = 1e-6
NEG = -30000.0
C_NAQ, C_NAK, C_NAV, C_DFQ, C_DFK, C_DFV, C_GA, C_GB = 0, 1024, 2048, 3072, 4096, 5120, 6144, 8192
DEBUG = False


class Op:
    __slots__ = ("e", "idx", "fn", "deps", "dma", "flag", "cum")

    def __init__(self, e, idx, fn, deps, dma):
        self.e, self.idx, self.fn, self.deps, self.dma = e, idx, fn, deps, dma
        self.flag = False
        self.cum = 0


class Sched:
    def __init__(self, nc, stack):
        self.nc = nc
        self.esem = {e: stack.enter_context(nc.semaphore("s_" + e)) for e in ENG}
        self.psem = stack.enter_context(nc.semaphore("s_phase"))
        self.stack = stack
        self.dsem = {}
        self.dma_cnt = {}
        self.ecum = {e: 0 for e in ENG}
        self.waited = {e: {} for e in ENG}
        self.phase = 0
        self.nops = 0
        self.nwaits = 0
        self.reset()

    def reset(self):
        self.ops = {e: [] for e in ENG}
        self.buf = {}
        self.phase_dma = set()

    def add(self, e, fn, r=(), w=(), dma=None):
        idx = len(self.ops[e])
        deps = set()
        for k in r:
            st = self.buf.get(k)
            if st is not None and st[0] is not None:
                deps.add(st[0])
        for k in w:
            st = self.buf.get(k)
            if st is not None:
                if st[0] is not None:
                    deps.add(st[0])
                deps.update(st[1].values())
        if dma is not None:
            self.dma_cnt[dma] = self.dma_cnt.get(dma, 0) + 16
            ref = ("dma", dma, self.dma_cnt[dma])
            rkey = ("dma", dma)
            self.phase_dma.add(dma)
        else:
            ref = ("eng", e, idx)
            rkey = ("eng", e)
        if e == "pe":
            deps = {d for d in deps if not (d[0] == "eng" and d[1] == "pe")}
        op = Op(e, idx, fn, deps, dma)
        for k in r:
            st = self.buf.setdefault(k, [None, {}])
            st[1][rkey] = ref
        for k in w:
            self.buf[k] = [ref, {}]
        self.ops[e].append(op)
        self.nops += 1
        return op

    def emit(self):
        nc = self.nc
        for e in ENG:
            for op in self.ops[e]:
                for d in op.deps:
                    if d[0] == "eng":
                        self.ops[d[1]][d[2]].flag = True
            last = None
            for op in self.ops[e]:
                if op.dma is None:
                    last = op
            if last is not None:
                last.flag = True
        for e in ENG:
            c = self.ecum[e]
            for op in self.ops[e]:
                if op.flag and op.dma is None:
                    c += 1
                op.cum = c
            self.ecum[e] = c
        for k in sorted(self.phase_dma, key=str):
            if k not in self.dsem:
                self.dsem[k] = self.stack.enter_context(nc.semaphore("d%d" % len(self.dsem)))
        phase = self.phase

        def run(e, eng):
            waited = self.waited[e]
            if phase > 0:
                eng.wait_ge(self.psem, phase)
            for op in self.ops[e]:
                need = {}
                for d in op.deps:
                    if d[0] == "eng":
                        sem = self.esem[d[1]]
                        val = self.ops[d[1]][d[2]].cum
                        key = ("e", d[1])
                    else:
                        sem = self.dsem[d[1]]
                        val = d[2]
                        key = ("d", d[1])
                    if val > need.get(key, (None, 0))[1]:
                        need[key] = (sem, val)
                for key, (sem, val) in need.items():
                    if waited.get(key, 0) >= val:
                        continue
                    waited[key] = val
                    eng.wait_ge(sem, val)
                    self.nwaits += 1
                ins = op.fn(eng)
                if op.dma is not None:
                    ins.then_inc(self.dsem[op.dma], 16)
                elif op.flag:
                    ins.then_inc(self.esem[e], 1)
            if e == "sp":
                for e2 in ENG:
                    if e2 != "sp" and self.ecum[e2] > 0:
                        eng.wait_ge(self.esem[e2], self.ecum[e2])
                for k in sorted(self.phase_dma, key=str):
                    eng.wait_ge(self.dsem[k], self.dma_cnt[k])
                eng.nop().then_inc(self.psem, 1)

        with nc.Block() as block:
            @block.tensor
            def _(eng):
                run("pe", eng)

            @block.scalar
            def _(eng):
                run("act", eng)

            @block.vector
            def _(eng):
                run("dve", eng)

            @block.gpsimd
            def _(eng):
                run("pool", eng)

            @block.sync
            def _(eng):
                run("sp", eng)
        self.phase += 1
        self.reset()


class Rot:
    def __init__(self, items):
        self.items = list(items)
        self.i = 0

    def next(self):
        v = self.items[self.i % len(self.items)]
        self.i += 1
        return v


def build_program():
    nc = bass.Bass("TRN2", target_bir_lowering=False)
    din = lambda name, shape, dt=F32: nc.dram_tensor(name, list(shape), dt, kind="ExternalInput").ap()
    skind = "ExternalOutput" if DEBUG else "Internal"
    dscr = lambda name, shape, dt: nc.dram_tensor(name, list(shape), dt, kind=skind).ap()

    xr = din("xr", [S_ALL, D])
    pr = din("pr", [OWN, 256])
    w_in = din("w_in", [D, 10240])
    w_na_out = din("w_na_out", [1024, D])
    w_df_out = din("w_df_out", [1024, D])
    w_o = din("w_o", [D, D])
    w_gate = din("w_gate", [D, HID])
    w_up = din("w_up", [D, HID])
    w_down = din("w_down", [HID, D])
    w_pg = din("w_ple_gate", [D, D])
    w_pp = din("w_ple_proj", [256, D])
    gcols_d = din("gcols", [128, 48])
    gvec_d = din("gvec", [128, 8])
    lamv_d = din("lamv", [128, 256])
    ident_d = din("ident", [128, 128])
    rotT_d = din("rotT", [128, 128])
    ct_d = din("ropec", [128, S_ALL])
    st_d = din("ropes", [128, S_ALL])
    t2_d = din("t2", [128, 8 * 15 * 64])
    rm_d = din("rowmask", [128, 16 * 6 * 64])
    out = nc.dram_tensor("out", [OWN, D], F32, kind="ExternalOutput").ap()

    KNA = dscr("KNA", [8, 128, 1536], BF16)
    QNA = dscr("QNA", [8, 128, OWN], BF16)
    VNA = dscr("VNA", [8, 1536, 128], BF16)
    KDF = dscr("KDF", [8, 128, S_ALL], BF16)
    QDF = dscr("QDF", [8, 128, OWN], BF16)
    VDF = dscr("VDF", [8, 128, 32, 128], BF16)
    X1 = dscr("X1", [OWN, D], F32)
    X2 = dscr("X2", [OWN, D], F32)
    if DEBUG:
        OAT = dscr("OAT", [128, 8, OWN], BF16)
        OBT = dscr("OBT", [128, 8, OWN], BF16)
        MT = dscr("MT", [128, 16, OWN], BF16)

    with contextlib.ExitStack() as st0:
        S = Sched(nc, st0)

        def sb(stack, name, shape, dt):
            return stack.enter_context(nc.sbuf_tensor("sb_" + name, list(shape), dt))

        ps_all = st0.enter_context(nc.psum_tensor("ps_all", [128, 4096], F32))
        ps_bf = ps_all.bitcast(BF16)
        PS = [ps_all[:, i * 512:(i + 1) * 512] for i in range(8)]
        PSB = [ps_bf[:, i * 1024:(i + 1) * 1024] for i in range(8)]
        pk = lambda i: ("ps", i)

        ident_f = sb(st0, "ident_f", [128, 128], F32)
        rot_f = sb(st0, "rot_f", [128, 128], F32)
        ident_b = sb(st0, "ident_b", [128, 128], BF16)
        rotT_b = sb(st0, "rotT_b", [128, 128], BF16)
        ones_b = sb(st0, "ones_b", [128, 128], BF16)
        bones_b = sb(st0, "bones_b", [128, 128], BF16)
        ones_f = sb(st0, "ones_f", [128, 128], F32)
        eps_c = sb(st0, "eps_c", [128, 1], F32)
        gcols = sb(st0, "gcols", [128, 48], F32)
        gvec = sb(st0, "gvec", [128, 8], F32)
        gsub08 = sb(st0, "gsub08", [128, 1], F32)
        lamv = sb(st0, "lamv", [128, 256], F32)
        lamw = sb(st0, "lamw", [128, 128], F32)
        lams = sb(st0, "lams", [128, 4], F32)
        nlam = sb(st0, "nlam", [128, 1], F32)

        def dma(e, out_ap, in_ap, r=(), w=(), key=None):
            S.add(e, lambda eng: eng.dma_start(out=out_ap, in_=in_ap), r=r, w=w, dma=key)

        dma("sp", ident_f[:], ident_d[:, :], w=["c"], key="c")
        dma("sp", rot_f[:], rotT_d[:, :], w=["c"], key="c")
        dma("sp", gcols[:], gcols_d[:, :], w=["c"], key="c")
        dma("sp", gvec[:], gvec_d[:, :], w=["c"], key="c")
        dma("sp", lamv[:], lamv_d[:, :], w=["c"], key="c")
        S.add("dve", lambda e: e.memset(ones_f[:], 1.0), w=["ones_f"])
        S.add("dve", lambda e: e.memset(eps_c[:], EPS), w=["eps"])
        S.add("dve", lambda e: e.memset(bones_b[:], 0.0), w=["bones"])
        S.add("dve", lambda e: e.tensor_copy(out=ident_b[:], in_=ident_f[:]), r=["c"], w=["k1"])
        S.add("dve", lambda e: e.tensor_copy(out=rotT_b[:], in_=rot_f[:]), r=["c"], w=["k2"])
        S.add("dve", lambda e: e.tensor_copy(out=ones_b[:], in_=ones_f[:]), r=["ones_f"], w=["k3"])
        S.add("dve", lambda e: e.tensor_copy(out=bones_b[0:64, 0:64], in_=ones_f[0:64, 0:64]), r=["ones_f", "bones"], w=["bones"])
        S.add("dve", lambda e: e.tensor_copy(out=bones_b[64:128, 64:128], in_=ones_f[64:128, 64:128]), r=["ones_f", "bones"], w=["bones"])
        S.add("dve", lambda e: e.tensor_scalar(out=gsub08[:], in0=gvec[:, 4:5], scalar1=0.8, scalar2=None, op0=ALU.mult), r=["c"], w=["k5"])
        S.add("dve", lambda e: e.tensor_tensor(out=lamw[:, 0:64], in0=lamv[:, 0:64], in1=lamv[:, 64:128], op=ALU.mult), r=["c"], w=["lw0"])
        S.add("dve", lambda e: e.tensor_tensor(out=lamw[:, 64:128], in0=lamv[:, 128:192], in1=lamv[:, 192:256], op=ALU.mult), r=["c"], w=["lw1"])
        S.add("dve", lambda e: e.reduce_sum(out=lams[:, 0:1], in_=lamw[:, 0:64], axis=AX.X), r=["lw0"], w=["ls0"])
        S.add("dve", lambda e: e.reduce_sum(out=lams[:, 1:2], in_=lamw[:, 64:128], axis=AX.X), r=["lw1"], w=["ls1"])
        S.add("act", lambda e: e.activation(out=lams[:, 2:4], in_=lams[:, 0:2], func=AF.Exp), r=["ls0", "ls1"], w=["ls2"])
        S.add("dve", lambda e: e.tensor_tensor(out=nlam[:], in0=lams[:, 3:4], in1=lams[:, 2:3], op=ALU.subtract), r=["ls2"], w=["nlam"])
        S.add("dve", lambda e: e.tensor_scalar(out=nlam[:], in0=nlam[:], scalar1=-0.2, scalar2=None, op0=ALU.add), r=["nlam"], w=["nlam"])

        def mm(outp, lhsT, rhs, start, stop, r, w):
            S.add("pe", lambda e: e.matmul(outp, lhsT=lhsT, rhs=rhs, start=start, stop=stop), r=r, w=w)

        def norm_phase(stack, src_tile, ntiles, gbase, dst, hkey, tag, use_pool=False, nxt=3):
            xt = [sb(stack, "%s_xt%d" % (tag, i), [128, D], F32) for i in range(nxt)]
            junk = sb(stack, tag + "_junk", [128, D], BF16)
            xh = [sb(stack, "%s_xh%d" % (tag, i), [128, D], BF16) for i in range(2)]
            ssq = sb(stack, tag + "_ssq", [128, ntiles], F32)
            rst = sb(stack, tag + "_rst", [128, ntiles], F32)
            gt = sb(stack, tag + "_gt", [128, 16, 128], F32)
            for c in range(16):
                S.add("dve", lambda e, c=c: e.tensor_scalar(out=gt[:, c, :], in0=ones_f[:], scalar1=gcols[:, gbase + c:gbase + c + 1],
                                                            scalar2=None, op0=ALU.mult), r=["c", "ones_f"], w=[("gt", c)])
            gtk = [("gt", c) for c in range(16)]
            banks = Rot([(0, 1), (2, 3)])
            NX = len(xt)

            def load(i):
                dma("sp", xt[i % NX][:], src_tile(i), w=[("xt", i % NX)], key=("xt", i % NX))

            def square(i):
                bx = i % NX
                S.add("act", lambda e: e.activation(out=junk[:], in_=xt[bx][:], func=AF.Square, accum_out=ssq[:, i:i + 1]),
                      r=[("xt", bx)], w=["junk", ("ssq", i)])

            for i in range(min(NX - 1, ntiles)):
                load(i)
            square(0)
            for i in range(ntiles):
                bx, b2 = i % NX, i % 2
                if i + NX - 1 < ntiles:
                    load(i + NX - 1)
                S.add("act", lambda e, i=i: e.activation(out=rst[:, i:i + 1], in_=ssq[:, i:i + 1], func=AF.Ln, bias=eps_c[:], scale=1.0 / D),
                      r=[("ssq", i), "eps"], w=[("rst", i)])
                if i + 1 < ntiles:
                    square(i + 1)
                S.add("act", lambda e, i=i: e.activation(out=rst[:, i:i + 1], in_=rst[:, i:i + 1], func=AF.Exp, scale=-0.5), r=[("rst", i)], w=[("rst", i)])
                if use_pool:
                    S.add("pool", lambda e, bx=bx, b2=b2, i=i: e.tensor_scalar(out=xh[b2][:], in0=xt[bx][:], scalar1=rst[:, i:i + 1],
                                                                               scalar2=1.0, op0=ALU.mult, op1=ALU.mult),
                          r=[("xt", bx), ("rst", i)], w=[("xh", b2)])
                else:
                    S.add("act", lambda e, bx=bx, b2=b2, i=i: e.activation(out=xh[b2][:], in_=xt[bx][:], func=AF.Copy, scale=rst[:, i:i + 1]),
                          r=[("xt", bx), ("rst", i)], w=[("xh", b2)])
                ba, bb = banks.next()
                for c in range(16):
                    bk = ba if c < 8 else bb
                    S.add("pe", lambda e, c=c, bk=bk, b2=b2: e.transpose(out=PSB[bk][:, (c % 8) * 128:(c % 8 + 1) * 128],
                                                                          in_=xh[b2][:, c * 128:(c + 1) * 128], identity=ident_b[:]),
                          r=[("xh", b2), "k1"], w=[pk(bk)])
                for hf, bk in ((0, ba), (1, bb)):
                    S.add("dve", lambda e, hf=hf, bk=bk, i=i: e.tensor_tensor(out=dst(hf, i), in0=PSB[bk].rearrange("p (c t) -> p c t", c=8),
                                                                             in1=gt[:, hf * 8:(hf + 1) * 8, :], op=ALU.mult),
                          r=[pk(bk)] + gtk, w=[hkey(i)])

        def wload(dst_ap, w_dram, k0, kn, c0, cn, key):
            src = w_dram[k0:k0 + kn, c0:c0 + cn].rearrange("(c p) n -> p c n", p=128)
            dma("pool", dst_ap, src, w=[key], key=key)

        stA0 = contextlib.ExitStack()
        hT_own = sb(stA0, "hT_own", [128, 16, OWN], BF16)
        with contextlib.ExitStack() as stA:
            hT_oth = sb(stA, "hT_oth", [128, 16, S_ALL - OWN], BF16)

            def hdst(hf, i):
                if i < 8:
                    return hT_own[:, hf * 8:(hf + 1) * 8, i * 128:(i + 1) * 128]
                return hT_oth[:, hf * 8:(hf + 1) * 8, (i - 8) * 128:(i - 7) * 128]

            wb = [sb(stA, "wb%d" % i, [128, 16, 256], BF16) for i in range(2)]
            with contextlib.ExitStack() as stA1:
                dma("pool", wb[0][:], w_in[0:D, C_NAQ:C_NAQ + 256].rearrange("(c p) n -> p c n", p=128), w=[("wb", 0)], key=("wb", 0))
                norm_phase(stA1, lambda i: xr[i * 128:(i + 1) * 128, :], 32, 0, hdst, lambda i: ("h", i), "a1", use_pool=True, nxt=4)
                S.emit()

            def hT(c, t0, n):
                if t0 < OWN:
                    return hT_own[:, c, t0:t0 + n]
                return hT_oth[:, c, t0 - OWN:t0 - OWN + n]

            with contextlib.ExitStack() as stA2:
                Ct = sb(stA2, "Ct", [128, S_ALL], F32)
                St = sb(stA2, "St", [128, S_ALL], F32)
                preloaded = [True]
                sq = [sb(stA2, "sq%d" % i, [128, 512], BF16) for i in range(3)]
                sd = [sb(stA2, "sd%d" % i, [128, 512], F32) for i in range(3)]
                Gb = [sb(stA2, "G%d" % i, [128, 512], BF16) for i in range(3)]
                Ab = [sb(stA2, "A%d" % i, [128, 512], BF16) for i in range(3)]
                Bb = [sb(stA2, "B%d" % i, [128, 512], BF16) for i in range(3)]
                yst = [sb(stA2, "yst%d" % i, [128, 512], BF16) for i in range(3)]
                dma("sp", Ct[:], ct_d[:, :], w=["Ct"], key="Ct")
                dma("sp", St[:], st_d[:, :], w=["St"], key="St")
                accb = Rot([0, 1, 2, 3])
                auxb = Rot([4, 5, 6, 7])
                wrot = Rot([0, 1])
                wk = Rot([0, 1, 2])
                yk = Rot([0, 1, 2])

                def post_norm(pbank, nt, gi, dkn, dest, rope, t0):
                    k2 = wk.next()
                    y = yk.next()
                    st = {}

                    def s0():
                        S.add("act", lambda e: e.activation(out=sq[k2][:, :nt], in_=PS[pbank][:, :nt], func=AF.Square), r=[pk(pbank)], w=[("sq", k2)])
                        if rope:
                            S.add("act", lambda e: e.activation(out=Gb[k2][:, :nt], in_=PS[pbank][:, :nt], func=AF.Copy, scale=gvec[:, gi:gi + 1]),
                                  r=[pk(pbank)], w=[("G", k2)])
                        ax = auxb.next()
                        st["ax"] = ax
                        mm(PS[ax][:, :nt], bones_b[:] if rope else ones_b[:], sq[k2][:, :nt], True, True, r=[("sq", k2)], w=[pk(ax)])
                        if rope:
                            S.add("dve", lambda e: e.tensor_tensor(out=Ab[k2][:, :nt], in0=Gb[k2][:, :nt], in1=Ct[:, t0:t0 + nt], op=ALU.mult),
                                  r=[("G", k2), "Ct"], w=[("A", k2)])
                            S.add("dve", lambda e: e.tensor_tensor(out=Bb[k2][:, :nt], in0=Gb[k2][:, :nt], in1=St[:, t0:t0 + nt], op=ALU.mult),
                                  r=[("G", k2), "St"], w=[("B", k2)])

                    def s1():
                        ax = st["ax"]
                        S.add("act", lambda e: e.activation(out=sd[k2][:, :nt], in_=PS[ax][:, :nt], func=AF.Ln, bias=eps_c[:], scale=1.0 / dkn),
                              r=[pk(ax)], w=[("sd", k2)])
                        S.add("act", lambda e: e.activation(out=sd[k2][:, :nt], in_=sd[k2][:, :nt], func=AF.Exp, scale=-0.5), r=[("sd", k2)], w=[("sd", k2)])
                        if not rope:
                            S.add("dve", lambda e: e.scalar_tensor_tensor(out=yst[y][:, :nt], in0=PS[pbank][:, :nt], scalar=gvec[:, gi:gi + 1],
                                                                          in1=sd[k2][:, :nt], op0=ALU.mult, op1=ALU.mult),
                                  r=[pk(pbank), ("sd", k2)], w=[("yst", y)])
                        else:
                            ax2 = auxb.next()
                            mm(PS[ax2][:, :nt], ident_b[:], Ab[k2][:, :nt], True, False, r=[("A", k2)], w=[pk(ax2)])
                            mm(PS[ax2][:, :nt], rotT_b[:], Bb[k2][:, :nt], False, True, r=[("B", k2)], w=[pk(ax2)])
                            S.add("dve", lambda e: e.tensor_tensor(out=yst[y][:, :nt], in0=PS[ax2][:, :nt], in1=sd[k2][:, :nt], op=ALU.mult),
                                  r=[pk(ax2), ("sd", k2)], w=[("yst", y)])
                        dma("sp", dest, yst[y][:, :nt], r=[("yst", y)], key=("yst", y))

                    return [s0, s1]

                pipe = []

                def pipe_step(new_stages=None):
                    for stages in pipe:
                        if stages:
                            stages.pop(0)()
                    while pipe and not pipe[0]:
                        pipe.pop(0)
                    if new_stages is not None:
                        pipe.append(list(new_stages))

                def pipe_drain():
                    while pipe:
                        pipe_step(None)

                def hkeys(t0, n):
                    return [("h", i) for i in range(t0 // 128, (t0 + n + 127) // 128)]

                def fm_proj(colbase, blocks, gi, dkn, destf, rope):
                    for b256 in range(4):
                        wi = wrot.next()
                        if preloaded[0]:
                            preloaded[0] = False
                        else:
                            wload(wb[wi][:], w_in, 0, D, colbase + b256 * 256, 256, ("wb", wi))
                        for half in range(2):
                            n = b256 * 2 + half
                            for (t0, nt, d0) in blocks:
                                bank = accb.next()
                                for c in range(KC):
                                    mm(PS[bank][:, :nt], wb[wi][:, c, half * 128:(half + 1) * 128], hT(c, t0, nt), c == 0, c == KC - 1,
                                       r=[("wb", wi)] + hkeys(t0, nt), w=[pk(bank)])
                                pipe_step(post_norm(bank, nt, gi, dkn, destf(n, d0, nt), rope, t0))

                vst = [sb(stA2, "vst%d" % i, [128, 512], BF16) for i in range(2)]
                vk = Rot([0, 1])

                def tm_proj(colbase, tiles, destf):
                    for b256 in range(4):
                        wi = wrot.next()
                        wload(wb[wi][:], w_in, 0, D, colbase + b256 * 256, 256, ("wb", wi))
                        for p0 in range(0, len(tiles), 2):
                            bank = accb.next()
                            for tt in range(2):
                                t0 = tiles[p0 + tt]
                                for c in range(KC):
                                    mm(PS[bank][:, tt * 256:(tt + 1) * 256], hT(c, t0, 128), wb[wi][:, c, :], c == 0, c == KC - 1,
                                       r=[("wb", wi)] + hkeys(t0, 128), w=[pk(bank)])
                            v = vk.next()
                            S.add("act", lambda e, bank=bank, v=v: e.copy(out=vst[v][:], in_=PS[bank]), r=[pk(bank)], w=[("vst", v)])
                            for tt in range(2):
                                dma("sp", destf(p0 + tt, b256), vst[v][:, tt * 256:(tt + 1) * 256].rearrange("p (h d) -> p h d", h=2),
                                    r=[("vst", v)], key=("vst", v))

                na_blocks = [(S_ALL - 256, 256, 0), (0, 512, 256), (512, 512, 768), (1024, 256, 1280)]
                own_blocks = [(0, 512, 0), (512, 512, 512)]
                all_blocks = [(t * 512, 512, t * 512) for t in range(8)]
                fm_proj(C_NAQ, own_blocks, 0, 128, lambda n, d0, nt: QNA[n, :, d0:d0 + nt], False)
                fm_proj(C_NAK, na_blocks, 1, 128, lambda n, d0, nt: KNA[n, :, d0:d0 + nt], False)
                na_tiles = [S_ALL - 256, S_ALL - 128] + [i * 128 for i in range(10)]
                pipe_drain()
                tm_proj(C_NAV, na_tiles, lambda m, b: VNA[2 * b:2 * b + 2, m * 128:(m + 1) * 128, :].rearrange("h p d -> p h d"))
                fm_proj(C_DFQ, own_blocks, 2, 64, lambda n, d0, nt: QDF[n, :, d0:d0 + nt], True)
                fm_proj(C_DFK, all_blocks, 3, 64, lambda n, d0, nt: KDF[n, :, d0:d0 + nt], True)
                pipe_drain()
                tm_proj(C_DFV, [i * 128 for i in range(32)], lambda m, b: VDF[2 * b:2 * b + 2, :, m, :].rearrange("h p d -> p h d"))
                S.emit()

        stBC = contextlib.ExitStack()
        oaT = sb(stBC, "oaT", [128, 8, OWN], BF16)
        obT = sb(stBC, "obT", [128, 8, OWN], BF16)

        with contextlib.ExitStack() as stB:
            T2 = sb(stB, "T2", [128, 8, 15, 64], F32)
            rmk = sb(stB, "rmk", [128, 16, 6, 64], F32)
            kT = [sb(stB, "nkT%d" % i, [128, 1536], BF16) for i in range(2)]
            qT = [sb(stB, "nqT%d" % i, [128, OWN], BF16) for i in range(2)]
            Ve = [sb(stB, "nVe%d" % i, [128, 12, 128], BF16) for i in range(2)]
            Vo = [sb(stB, "nVo%d" % i, [128, 11, 128], BF16) for i in range(2)]
            sbf = [sb(stB, "nsb%d" % i, [128, 6, 64], F32) for i in range(4)]
            Pn = [sb(stB, "nP%d" % i, [128, 6, 64], BF16) for i in range(4)]
            rcn = [sb(stB, "nrc%d" % i, [128, 512], F32) for i in range(2)]
            dma("sp", T2[:, 0, :, :].rearrange("p d c -> p (d c)"), t2_d[:, 0:960], w=["T2a"], key="T2a")
            dma("sp", rmk[:, 0:4, :, :].rearrange("p r c q -> p (r c q)"), rm_d[:, 0:4 * 384], w=["rmka"], key="rmka")
            dma("sp", T2[:, 1:8, :, :].rearrange("p h d c -> p (h d c)"), t2_d[:, 960:], w=["T2b"], key="T2b")
            dma("sp", rmk[:, 4:16, :, :].rearrange("p r c q -> p (r c q)"), rm_d[:, 4 * 384:], w=["rmkb"], key="rmkb")
            sbank = Rot([0, 1, 2, 3])
            obank = Rot([(4, 5), (6, 7)])
            sk = Rot([0, 1, 2, 3])
            na_scale = 128.0 ** -0.5
            pipe = []

            def na_unit(h, hb, half, rr, bo, bz):
                rl = half * 8 + rr
                if rl <= 3:
                    vlo, vhi = rl, 11
                elif rl <= 12:
                    vlo, vhi = rl, rl + 7
                else:
                    vlo, vhi = 12, rl + 7
                v0s = list(range(vlo, vhi + 1, 2))
                n = len(v0s)
                drp0 = vlo - rl + 3
                bs = sbank.next()
                k2 = sk.next()
                for ci, v0 in enumerate(v0s):
                    mm(PS[bs][:, ci * 64:(ci + 1) * 64], kT[hb][:, v0 * 64:v0 * 64 + 128], qT[hb][:, rl * 64:(rl + 1) * 64], True, True,
                       r=[("kT", hb), ("qT", hb)], w=[pk(bs)])
                S.add("dve", lambda e: e.scalar_tensor_tensor(
                    out=sbf[k2][:, 0:n, :], in0=PS[bs][:, 0:n * 64].rearrange("p (n c) -> p n c", n=n), scalar=na_scale,
                    in1=T2[:, h, drp0:drp0 + 2 * n - 1:2, :], op0=ALU.mult, op1=ALU.add),
                    r=[pk(bs), "T2a" if h == 0 else "T2b"], w=[("sbf", k2)])
                if not (4 <= rl <= 12):
                    S.add("dve", lambda e: e.tensor_tensor(out=sbf[k2][:, 0:n, :], in0=sbf[k2][:, 0:n, :], in1=rmk[:, rl, 0:n, :], op=ALU.add),
                          r=[("sbf", k2), "rmka" if rl < 4 else "rmkb"], w=[("sbf", k2)])
                S.add("act", lambda e: e.activation(out=Pn[k2][:, 0:n, :], in_=sbf[k2][:, 0:n, :], func=AF.Exp),
                      r=[("sbf", k2)], w=[("Pn", k2)])

                def back():
                    for ci, v0 in enumerate(v0s):
                        vt = Ve[hb][:, v0 // 2, :] if v0 % 2 == 0 else Vo[hb][:, (v0 - 1) // 2, :]
                        mm(PS[bo][:, rr * 64:(rr + 1) * 64], vt, Pn[k2][:, ci, :], ci == 0, ci == n - 1,
                           r=[("Pn", k2), ("Ve", hb), ("Vo", hb)], w=[pk(bo)])
                    for ci in range(n):
                        mm(PS[bz][:, rr * 64:(rr + 1) * 64], ones_b[:], Pn[k2][:, ci, :], ci == 0, ci == n - 1,
                           r=[("Pn", k2)], w=[pk(bz)])
                    if rr == 7:
                        r2 = (h * 2 + half) % 2

                        def fin():
                            S.add("act", lambda e: e.activation(out=rcn[r2][:], in_=PS[bz], func=AF.Ln), r=[pk(bz)], w=[("rcn", r2)])
                            S.add("act", lambda e: e.activation(out=rcn[r2][:], in_=rcn[r2][:], func=AF.Exp, scale=-1.0), r=[("rcn", r2)], w=[("rcn", r2)])
                            S.add("dve", lambda e: e.tensor_tensor(out=oaT[:, h, half * 512:(half + 1) * 512], in0=PS[bo],
                                                                   in1=rcn[r2][:], op=ALU.mult),
                                  r=[pk(bo), ("rcn", r2)], w=[("oaT", h, half)])
                        fins.append([4, fin])
                return back

            LAG = 3
            fins = []
            for h in range(8):
                hb = h % 2
                dma("sp", kT[hb][:], KNA[h, :, :], w=[("kT", hb)], key=("kT", hb))
                dma("sp", qT[hb][:], QNA[h, :, :], w=[("qT", hb)], key=("qT", hb))
                dma("sp", Ve[hb][:], VNA[h, :, :].rearrange("(m p) d -> p m d", p=128), w=[("Ve", hb)], key=("Ve", hb))
                dma("sp", Vo[hb][:], VNA[h, 64:64 + 11 * 128, :].rearrange("(m p) d -> p m d", p=128), w=[("Vo", hb)], key=("Vo", hb))
                for half in range(2):
                    bo, bz = obank.next()
                    for rr in range(8):
                        pipe.append(na_unit(h, hb, half, rr, bo, bz))
                        if len(pipe) > LAG:
                            pipe.pop(0)()
                        for f in fins:
                            f[0] -= 1
                        while fins and fins[0][0] <= 0:
                            fins.pop(0)[1]()
            while pipe:
                pipe.pop(0)()
            while fins:
                fins.pop(0)[1]()
            if DEBUG:
                dma("sp", OAT[:, :, :], oaT[:], r=[("oaT", h, hf) for h in range(8) for hf in range(2)], key="dbg")
            S.emit()

        with contextlib.ExitStack() as stC:
            kT = [sb(stC, "dkT%d" % i, [128, S_ALL], BF16) for i in range(2)]
            qz0 = [sb(stC, "dqz0%d" % i, [128, OWN], BF16) for i in range(2)]
            qz1 = [sb(stC, "dqz1%d" % i, [128, OWN], BF16) for i in range(2)]
            for i in range(2):
                S.add("dve", lambda e, i=i: e.memset(qz0[i][64:128, :], 0.0), w=[("qz0z", i)])
                S.add("dve", lambda e, i=i: e.memset(qz1[i][0:64, :], 0.0), w=[("qz1z", i)])
            Vd = [sb(stC, "dV%d" % i, [128, 32, 128], BF16) for i in range(2)]
            Pd = [sb(stC, "dP%d" % i, [128, 1024], BF16) for i in range(4)]
            r0 = sb(stC, "dr0", [128, 512], F32)
            r1 = sb(stC, "dr1", [128, 512], F32)
            t0b = sb(stC, "dt0", [128, 512], F32)
            a0s = sb(stC, "da0s", [128, 512], BF16)
            t1b = sb(stC, "dt1", [128, 512], F32)
            dd = sb(stC, "ddd", [128, 512], F32)
            sqd = sb(stC, "dsq", [128, 512], BF16)
            sdd = sb(stC, "dsd", [128, 512], F32)
            df_scale = 64.0 ** -0.5
            sbank = Rot([(0, 1), (2, 3)])
            pkk = Rot([0, 1, 2, 3])
            BO0, BO1, BZ0, BZ1 = 4, 5, 6, 7
            for h in range(8):
                hb = h % 2
                dma("sp", kT[hb][:], KDF[h, :, :], w=[("kT", hb)], key=("kT", hb))
                dma("sp", qz0[hb][0:64, :], QDF[h, 0:64, :], w=[("qz0", hb)], key=("qz0", hb))
                dma("sp", qz1[hb][64:128, :], QDF[h, 64:128, :], w=[("qz1", hb)], key=("qz1", hb))
                dma("sp", Vd[hb][:], VDF[h, :, :, :], w=[("Vd", hb)], key=("Vd", hb))
                for qb in range(2):
                    qs = slice(qb * 512, (qb + 1) * 512)

                    def qk(kc):
                        b0, b1 = sbank.next()
                        mm(PS[b0], kT[hb][:, kc * 128:(kc + 1) * 128], qz0[hb][:, qs], True, True,
                           r=[("kT", hb), ("qz0", hb), ("qz0z", hb)], w=[pk(b0)])
                        mm(PS[b1], kT[hb][:, kc * 128:(kc + 1) * 128], qz1[hb][:, qs], True, True,
                           r=[("kT", hb), ("qz1", hb), ("qz1z", hb)], w=[pk(b1)])
                        p = pkk.next()
                        S.add("act", lambda e, b0=b0, p=p: e.activation(out=Pd[p][:], in_=ps_all[:, b0 * 512:(b0 + 2) * 512], func=AF.Exp, scale=df_scale),
                              r=[pk(b0), pk(b1)], w=[("Pd", p)])
                        return p

                    def av(kc, p):
                        st_, sp_ = kc == 0, kc == 31
                        mm(PS[BO0], Vd[hb][:, kc, :], Pd[p][:, 0:512], st_, sp_, r=[("Pd", p), ("Vd", hb)], w=[pk(BO0)])
                        mm(PS[BO1], Vd[hb][:, kc, :], Pd[p][:, 512:1024], st_, sp_, r=[("Pd", p), ("Vd", hb)], w=[pk(BO1)])
                        mm(PS[BZ1], ones_b[:], Pd[p][:, 512:1024], st_, sp_, r=[("Pd", p)], w=[pk(BZ1)])
                        if st_:
                            S.add("dve", lambda e, p=p: e.tensor_copy(out=PS[BZ0], in_=Pd[p][:, 0:512]), r=[("Pd", p)], w=[pk(BZ0)])
                        else:
                            S.add("dve", lambda e, p=p: e.tensor_tensor(out=PS[BZ0], in0=PS[BZ0], in1=Pd[p][:, 0:512], op=ALU.add),
                                  r=[("Pd", p), pk(BZ0)], w=[pk(BZ0)])

                    pq = [qk(0), qk(1)]
                    for kc in range(32):
                        if kc + 2 < 32:
                            pq.append(qk(kc + 2))
                        av(kc, pq.pop(0))
                    S.add("dve", lambda e: e.tensor_copy(out=a0s[:], in_=PS[BZ0]), r=[pk(BZ0)], w=["a0s"])
                    mm(PS[BZ0], ones_b[:], a0s[:], True, True, r=["a0s"], w=[pk(BZ0)])
                    S.add("act", lambda e: e.activation(out=r1[:], in_=PS[BZ1], func=AF.Ln), r=[pk(BZ1)], w=["r1"])
                    S.add("act", lambda e: e.activation(out=r1[:], in_=r1[:], func=AF.Exp, scale=-1.0), r=["r1"], w=["r1"])
                    S.add("act", lambda e: e.activation(out=r0[:], in_=PS[BZ0], func=AF.Ln), r=[pk(BZ0)], w=["r0"])
                    S.add("act", lambda e: e.activation(out=r0[:], in_=r0[:], func=AF.Exp, scale=-1.0), r=["r0"], w=["r0"])
                    S.add("dve", lambda e: e.tensor_tensor(out=t1b[:], in0=PS[BO1], in1=r1[:], op=ALU.mult), r=[pk(BO1), "r1"], w=["t1"])
                    S.add("dve", lambda e: e.tensor_tensor(out=t0b[:], in0=PS[BO0], in1=r0[:], op=ALU.mult), r=[pk(BO0), "r0"], w=["t0"])
                    S.add("dve", lambda e: e.scalar_tensor_tensor(out=dd[:], in0=t1b[:], scalar=nlam[:, 0:1], in1=t0b[:], op0=ALU.mult, op1=ALU.add),
                          r=["t0", "t1"], w=["dd"])
                    S.add("act", lambda e: e.activation(out=sqd[:], in_=dd[:], func=AF.Square), r=["dd"], w=["sqd"])
                    mm(PS[BZ1], ones_b[:], sqd[:], True, True, r=["sqd"], w=[pk(BZ1)])
                    S.add("act", lambda e: e.activation(out=sdd[:], in_=PS[BZ1], func=AF.Ln, bias=eps_c[:], scale=1.0 / 128), r=[pk(BZ1)], w=["sdd"])
                    S.add("act", lambda e: e.activation(out=sdd[:], in_=sdd[:], func=AF.Exp, scale=-0.5), r=["sdd"], w=["sdd"])
                    S.add("dve", lambda e, h=h, qs=qs: e.scalar_tensor_tensor(out=obT[:, h, qs], in0=dd[:], scalar=gsub08[:, 0:1], in1=sdd[:],
                                                                           op0=ALU.mult, op1=ALU.mult),
                          r=["dd", "sdd"], w=[("obT", h, qb)])
            if DEBUG:
                dma("sp", OBT[:, :, :], obT[:], r=[("obT", h, hf) for h in range(8) for hf in range(2)], key="dbg")
            S.emit()

        def tm_residual(stack, tag, srcs, res_dram, dst_dram, gate=None, akeys=None, mid=None):
            wbufs = []
            for si, (aT, nk, wd) in enumerate(srcs):
                wbufs.append([sb(stack, "%s_w%d_%d" % (tag, si, i), [128, nk, 512], BF16) for i in range(2)])
            res = [sb(stack, "%s_res%d" % (tag, i), [128, 512], F32) for i in range(3)]
            ost = [sb(stack, "%s_ost%d" % (tag, i), [128, 512], F32) for i in range(3)]
            sg = [sb(stack, "%s_sg%d" % (tag, i), [128, 512], F32) for i in range(2)] if gate else None
            bankr = Rot([(0, 1), (2, 3), (4, 5), (6, 7)]) if gate else Rot([(i,) for i in range(8)])
            def res_load(u):
                nb_, tt_ = divmod(u, 8)
                k3_ = u % 3
                dma("sp", res[k3_][:], res_dram[tt_ * 128:(tt_ + 1) * 128, nb_ * 512:(nb_ + 1) * 512], w=[(tag + "res", k3_)], key=(tag + "res", k3_))

            for nb_ in range(2):
                for si, (aT, nk, wd) in enumerate(srcs):
                    wload(wbufs[si][nb_][:], wd, 0, nk * 128, nb_ * 512, 512, (tag + "w%d" % si, nb_))
            if mid is not None:
                mid()
            res_load(0)
            res_load(1)
            for nb in range(4):
                wi = nb % 2
                cs = slice(nb * 512, (nb + 1) * 512)
                if nb > 1:
                    for si, (aT, nk, wd) in enumerate(srcs):
                        wload(wbufs[si][wi][:], wd, 0, nk * 128, nb * 512, 512, (tag + "w%d" % si, wi))
                for tt in range(8):
                    tsl = slice(tt * 128, (tt + 1) * 128)
                    banks = bankr.next()
                    u = nb * 8 + tt
                    k3 = u % 3
                    if u + 2 < 32:
                        res_load(u + 2)
                    for si, (aT, nk, wd) in enumerate(srcs):
                        ak = list(akeys[si](tt)) if akeys else []
                        for c in range(nk):
                            mm(PS[banks[si]], aT[:, c, tsl], wbufs[si][wi][:, c, :], c == 0, c == nk - 1, r=[(tag + "w%d" % si, wi)] + ak, w=[pk(banks[si])])
                    if gate:
                        g2 = k3 % 2
                        S.add("act", lambda e, b=banks[0], g2=g2: e.activation(out=sg[g2][:], in_=PS[b], func=AF.Sigmoid), r=[pk(banks[0])], w=[(tag + "sg", g2)])
                        S.add("dve", lambda e, b=banks[1], g2=g2: e.tensor_tensor(out=sg[g2][:], in0=PS[b], in1=sg[g2][:], op=ALU.mult),
                              r=[pk(banks[1]), (tag + "sg", g2)], w=[(tag + "sg", g2)])
                        S.add("dve", lambda e, g2=g2, k3=k3: e.tensor_tensor(out=ost[k3][:], in0=sg[g2][:], in1=res[k3][:], op=ALU.add),
                              r=[(tag + "sg", g2), (tag + "res", k3)], w=[(tag + "ost", k3)])
                    else:
                        S.add("dve", lambda e, b=banks[0], k3=k3: e.tensor_tensor(out=ost[k3][:], in0=PS[b], in1=res[k3][:], op=ALU.add),
                              r=[pk(banks[0]), (tag + "res", k3)], w=[(tag + "ost", k3)])
                    dma("sp", dst_dram[tsl, cs], ost[k3][:], r=[(tag + "ost", k3)], key=(tag + "ost", k3))

        stD = contextlib.ExitStack()
        mT = sb(stD, "mT", [128, 16, OWN], BF16)
        with contextlib.ExitStack() as stD1:
            wna = [sb(stD1, "wna%d" % i, [128, 8, 128], BF16) for i in range(2)]
            wdf = [sb(stD1, "wdf%d" % i, [128, 8, 128], BF16) for i in range(2)]
            wga = [sb(stD1, "wga%d" % i, [128, 16, 128], BF16) for i in range(2)]
            wgb = [sb(stD1, "wgb%d" % i, [128, 16, 128], BF16) for i in range(2)]
            sga = [sb(stD1, "sga%d" % i, [128, 512], F32) for i in range(2)]
            sgb = [sb(stD1, "sgb%d" % i, [128, 512], F32) for i in range(2)]
            ta = [sb(stD1, "ta%d" % i, [128, 512], F32) for i in range(2)]
            tb = [sb(stD1, "tb%d" % i, [128, 512], F32) for i in range(2)]
            bankr = Rot([(0, 1, 2, 3), (4, 5, 6, 7)])
            kk = Rot([0, 1])
            for b128 in range(16):
                wi = b128 % 2
                c0 = b128 * 128
                wload(wna[wi][:], w_na_out, 0, 1024, c0, 128, ("wna", wi))
                wload(wdf[wi][:], w_df_out, 0, 1024, c0, 128, ("wdf", wi))
                wload(wga[wi][:], w_in, 0, D, C_GA + c0, 128, ("wga", wi))
                wload(wgb[wi][:], w_in, 0, D, C_GB + c0, 128, ("wgb", wi))
                for half in range(1):
                    n = b128
                    hs = slice(0, 128)
                    for tb_ in range(2):
                        ts_ = slice(tb_ * 512, (tb_ + 1) * 512)
                        bya, byb, bga, bgb = bankr.next()
                        for c in range(8):
                            mm(PS[bya], wna[wi][:, c, hs], oaT[:, c, ts_], c == 0, c == 7, r=[("wna", wi)], w=[pk(bya)])
                        for c in range(8):
                            mm(PS[byb], wdf[wi][:, c, hs], obT[:, c, ts_], c == 0, c == 7, r=[("wdf", wi)], w=[pk(byb)])
                        for c in range(16):
                            mm(PS[bga], wga[wi][:, c, hs], hT_own[:, c, ts_], c == 0, c == 15, r=[("wga", wi)], w=[pk(bga)])
                        for c in range(16):
                            mm(PS[bgb], wgb[wi][:, c, hs], hT_own[:, c, ts_], c == 0, c == 15, r=[("wgb", wi)], w=[pk(bgb)])
                        k2 = kk.next()
                        S.add("act", lambda e, bga=bga, k2=k2: e.activation(out=sga[k2][:], in_=PS[bga], func=AF.Sigmoid), r=[pk(bga)], w=[("sga", k2)])
                        S.add("act", lambda e, bgb=bgb, k2=k2: e.activation(out=sgb[k2][:], in_=PS[bgb], func=AF.Sigmoid), r=[pk(bgb)], w=[("sgb", k2)])
                        S.add("dve", lambda e, bya=bya, k2=k2: e.tensor_tensor(out=ta[k2][:], in0=PS[bya], in1=sga[k2][:], op=ALU.mult),
                              r=[pk(bya), ("sga", k2)], w=[("ta", k2)])
                        S.add("dve", lambda e, byb=byb, k2=k2: e.tensor_tensor(out=tb[k2][:], in0=PS[byb], in1=sgb[k2][:], op=ALU.mult),
                              r=[pk(byb), ("sgb", k2)], w=[("tb", k2)])
                        S.add("dve", lambda e, k2=k2, n=n, ts_=ts_: e.tensor_tensor(out=mT[:, n, ts_], in0=ta[k2][:], in1=tb[k2][:], op=ALU.add),
                              r=[("ta", k2), ("tb", k2)], w=[("mT", n, ts_.start)])
            if DEBUG:
                dma("sp", MT[:, :, :], mT[:], r=[("mT", n, t) for n in range(16) for t in (0, 512)], key="dbg")
            tm_residual(stD1, "d2", [(mT, 16, w_o)], xr[0:OWN, :], X1, akeys=[lambda tt: [("mT", n, (tt // 4) * 512) for n in range(16)]])
            S.emit()

        stD.close()
        stBC.close()
        stA0.close()

        with contextlib.ExitStack() as stE:
            actT = sb(stE, "actT", [128, HC, OWN], BF16)
            with contextlib.ExitStack() as stE12:
                hfT = sb(stE12, "hfT", [128, 16, OWN], BF16)
                with contextlib.ExitStack() as stE1:
                    stE2 = stE1
                    wg = [sb(stE2, "wg%d" % i, [128, 16, 256], BF16) for i in range(2)]
                    wu = [sb(stE2, "wu%d" % i, [128, 16, 256], BF16) for i in range(2)]
                    for pb_ in range(2):
                        wload(wg[pb_][:], w_gate, 0, D, pb_ * 256, 256, ("wg", pb_))
                        wload(wu[pb_][:], w_up, 0, D, pb_ * 256, 256, ("wu", pb_))
                    norm_phase(stE1, lambda i: X1[i * 128:(i + 1) * 128, :], 8, 16,
                               lambda hf, i: hfT[:, hf * 8:(hf + 1) * 8, i * 128:(i + 1) * 128], lambda i: ("h", i), "e1", use_pool=True)
                    sl = [sb(stE2, "sl%d" % i, [128, 512], F32) for i in range(2)]
                    bankr = Rot([(0, 1), (2, 3), (4, 5), (6, 7)])
                    kk = Rot([0, 1])
                    for b256 in range(22):
                        wi = b256 % 2
                        if b256 > 1:
                            wload(wg[wi][:], w_gate, 0, D, b256 * 256, 256, ("wg", wi))
                            wload(wu[wi][:], w_up, 0, D, b256 * 256, 256, ("wu", wi))
                        for half in range(2):
                            n = b256 * 2 + half
                            hs = slice(half * 128, (half + 1) * 128)
                            for tb_ in range(2):
                                ts_ = slice(tb_ * 512, (tb_ + 1) * 512)
                                bg, bu = bankr.next()
                                hk_ = [("h", i) for i in range(tb_ * 4, tb_ * 4 + 4)]
                                for c in range(16):
                                    mm(PS[bg], wg[wi][:, c, hs], hfT[:, c, ts_], c == 0, c == 15, r=[("wg", wi)] + hk_, w=[pk(bg)])
                                for c in range(16):
                                    mm(PS[bu], wu[wi][:, c, hs], hfT[:, c, ts_], c == 0, c == 15, r=[("wu", wi)] + hk_, w=[pk(bu)])
                                k2 = kk.next()
                                S.add("act", lambda e, bg=bg, k2=k2: e.activation(out=sl[k2][:], in_=PS[bg], func=AF.Silu), r=[pk(bg)], w=[("sl", k2)])
                                S.add("dve", lambda e, bu=bu, k2=k2, n=n, ts_=ts_: e.tensor_tensor(out=actT[:, n, ts_], in0=PS[bu], in1=sl[k2][:], op=ALU.mult),
                                      r=[pk(bu), ("sl", k2)], w=[("actT", n, ts_.start)])
                    S.emit()
            with contextlib.ExitStack() as stE3:
                wd_ = [sb(stE3, "wd%d" % i, [128, 4, 512], BF16) for i in range(3)]
                res = [sb(stE3, "e3res%d" % i, [128, 512], F32) for i in range(8)]
                ost = [sb(stE3, "e3ost%d" % i, [128, 512], F32) for i in range(3)]
                wr = Rot([0, 1, 2])
                rk = Rot([0, 1, 2])
                for nb in range(4):
                    cs = slice(nb * 512, (nb + 1) * 512)
                    for tt in range(8):
                        dma("sp", res[tt][:], X1[tt * 128:(tt + 1) * 128, cs], w=[("e3res", tt)], key=("e3res", tt))
                    for kg in range(11):
                        wi = wr.next()
                        wload(wd_[wi][:], w_down, kg * 512, 512, nb * 512, 512, ("wd", wi))
                        if kg in (0, 10):
                            order = [(kq, tt) for tt in range(8) for kq in range(4)]
                        else:
                            order = [(kq, tt) for kq in range(4) for tt in range(8)]
                        for kq, tt in order:
                            kc = kg * 4 + kq
                            mm(PS[tt], actT[:, kc, tt * 128:(tt + 1) * 128], wd_[wi][:, kq, :], kc == 0, kc == HC - 1, r=[("wd", wi)], w=[pk(tt)])
                    for tt in range(8):
                        tsl = slice(tt * 128, (tt + 1) * 128)
                        k3 = rk.next()
                        S.add("dve", lambda e, tt=tt, k3=k3: e.tensor_tensor(out=ost[k3][:], in0=PS[tt], in1=res[tt][:], op=ALU.add),
                              r=[pk(tt), ("e3res", tt)], w=[("e3ost", k3)])
                        dma("sp", X2[tsl, cs], ost[k3][:], r=[("e3ost", k3)], key=("e3ost", k3))
                S.emit()

        with contextlib.ExitStack() as stF:
            hpT = sb(stF, "hpT", [128, 16, OWN], BF16)
            pT = sb(stF, "pT", [128, 2, OWN], BF16)
            with contextlib.ExitStack() as stF1:
                pt = [sb(stF1, "pt%d" % i, [128, 256], F32) for i in range(2)]
                pb = [sb(stF1, "pb%d" % i, [128, 256], BF16) for i in range(2)]

                def f_mid():
                    norm_phase(stF1, lambda i: X2[i * 128:(i + 1) * 128, :], 8, 32,
                               lambda hf, i: hpT[:, hf * 8:(hf + 1) * 8, i * 128:(i + 1) * 128], lambda i: ("h", i), "f1", use_pool=True)
                    for i in range(8):
                        b2 = i % 2
                        dma("sp", pt[b2][:], pr[i * 128:(i + 1) * 128, :], w=[("pt", b2)], key=("pt", b2))
                        S.add("act", lambda e, b2=b2: e.copy(out=pb[b2][:], in_=pt[b2][:]), r=[("pt", b2)], w=[("pb", b2)])
                        bk = 4 + b2
                        for c in range(2):
                            S.add("pe", lambda e, c=c, bk=bk, b2=b2: e.transpose(out=PSB[bk][:, c * 128:(c + 1) * 128], in_=pb[b2][:, c * 128:(c + 1) * 128],
                                                                                  identity=ident_b[:]), r=[("pb", b2)], w=[pk(bk)])
                        S.add("dve", lambda e, bk=bk, i=i: e.tensor_copy(out=pT[:, :, i * 128:(i + 1) * 128],
                                                                         in_=PSB[bk][:, 0:256].rearrange("p (c t) -> p c t", c=2)),
                              r=[pk(bk)], w=[("pT", i)])

                tm_residual(stF1, "f2", [(hpT, 16, w_pg), (pT, 2, w_pp)], X2, out, gate=True,
                            akeys=[lambda tt: [("h", tt)], lambda tt: [("pT", tt)]], mid=f_mid)
                S.emit()
        print("bass program: ops=%d waits=%d dma_sems=%d" % (S.nops, S.nwaits, len(S.dsem)))
    return nc


_NC_CACHE = {}


def _host_consts():
    ident = np.eye(128, dtype=np.float32)
    R = np.zeros((128, 128), np.float32)
    for p in range(128):
        if p % 64 < 32:
            R[p, p + 32] = -1.0
        else:
            R[p, p - 32] = 1.0
    rotT = np.ascontiguousarray(R.T)
    inv = (1.0 / (10000.0 ** (np.arange(0, 64, 2, dtype=np.float32) / np.float32(64)))).astype(np.float32)
    ang = np.arange(S_ALL, dtype=np.float32)[:, None] * inv[None, :]
    cos = np.cos(ang).astype(np.float32)
    sin = np.sin(ang).astype(np.float32)
    return ident, rotT, cos, sin


def _t2_table(rpb):
    c = np.arange(64)
    cs = np.clip(c - 8, 0, 48)
    kc = np.arange(64)
    inwin = (kc[:, None] >= cs[None, :]) & (kc[:, None] < cs[None, :] + 16)
    dc = np.clip(kc[:, None] - c[None, :] + 15, 0, 30)
    t2 = np.full((2, 64, 8, 15, 64), NEG, np.float32)
    for a in range(2):
        for drp in range(15):
            dr = drp + a
            if dr > 14:
                continue
            vals = rpb[:, dr, :][:, dc]
            vals = np.where(inwin[None], vals, np.float32(NEG))
            t2[a, :, :, drp, :] = vals.transpose(1, 0, 2)
    return np.ascontiguousarray(t2.reshape(128, 8 * 15 * 64))


def _rowmask(j):
    rm = np.zeros((2, 64, 16, 6), np.float32)
    for rl in range(16):
        r = 16 * j + rl
        rs = min(max(r - 4, 0), 56)
        if rl <= 3:
            vlo, vhi = rl, 11
        elif rl <= 12:
            vlo, vhi = rl, rl + 7
        else:
            vlo, vhi = 12, rl + 7
        for ci, v0 in enumerate(range(vlo, vhi + 1, 2)):
            for a in range(2):
                rho = 16 * j - 4 + v0 + a
                ok = rs <= rho <= rs + 7
                rm[a, :, rl, ci] = 0.0 if ok else NEG
    return np.ascontiguousarray(np.broadcast_to(rm.reshape(128, 96, 1), (128, 96, 64)).reshape(128, 96 * 64))


def kernel(x, p, g_mix, w_in, g_na_q, g_na_k, na_rpb, g_df_q, g_df_k, lam_q1, lam_k1, lam_q2, lam_k2,
           g_df_sub, w_na_out, w_df_out, w_o, g_ffn, w_gate, w_up, w_down, g_ple, w_ple_gate, w_ple_proj):
    f = lambda a: np.ascontiguousarray(np.asarray(a, dtype=np.float32))
    x = f(x); p = f(p)
    if "nc" not in _NC_CACHE:
        _NC_CACHE["nc"] = build_program()
    nc = _NC_CACHE["nc"]
    ident, rotT, cos, sin = _host_consts()
    col = lambda g: f(g).reshape(16, 128).T
    gcols = np.ascontiguousarray(np.concatenate([col(g_mix[0]), col(g_ffn[0]), col(g_ple[0])], axis=1))
    gvec = np.zeros((128, 8), np.float32)
    gvec[:, 0] = f(g_na_q[0]); gvec[:, 1] = f(g_na_k[0])
    gvec[:, 2] = np.tile(f(g_df_q[0]), 2); gvec[:, 3] = np.tile(f(g_df_k[0]), 2)
    gvec[:, 4] = f(g_df_sub[0])
    lamv = np.ascontiguousarray(np.broadcast_to(np.concatenate([f(lam_q1[0]), f(lam_k1[0]), f(lam_q2[0]), f(lam_k2[0])])[None, :], (128, 256)))
    t2 = _t2_table(f(na_rpb[0]))
    shared = {
        "w_in": f(w_in[0]), "w_na_out": f(w_na_out[0]), "w_df_out": f(w_df_out[0]), "w_o": f(w_o[0]),
        "w_gate": f(w_gate[0]), "w_up": f(w_up[0]), "w_down": f(w_down[0]), "w_ple_gate": f(w_ple_gate[0]),
        "w_ple_proj": f(w_ple_proj[0]), "gcols": gcols, "gvec": gvec, "lamv": lamv, "ident": ident, "rotT": rotT, "t2": t2,
    }
    in_maps = []
    for core in range(8):
        b, j = core // 4, core % 4
        pos = (np.arange(S_ALL) + OWN * j) % S_ALL
        m = dict(shared)
        m["xr"] = np.ascontiguousarray(x[b][pos])
        m["pr"] = np.ascontiguousarray(p[0, b, OWN * j:OWN * (j + 1)])
        m["ropec"] = np.ascontiguousarray(np.tile(cos[pos].T, (4, 1)))
        m["ropes"] = np.ascontiguousarray(np.tile(sin[pos].T, (4, 1)))
        m["rowmask"] = _rowmask(j)
        in_maps.append(m)
    res = run_bass_kernel_spmd(nc, in_maps, core_ids=list(range(8)))
    _NC_CACHE["last"] = res
    outp = np.empty((2, S_ALL, D), np.float32)
    for core in range(8):
        b, j = core // 4, core % 4
        outp[b, OWN * j:OWN * (j + 1)] = res.results[core]["out"]
    return outp
```

```python
import contextlib
import math
import numpy as np
import concourse.bass as bass
import concourse.mybir as mybir
from concourse.bass_utils import run_bass_kernel_spmd

F32 = mybir.dt.float32
BF16 = mybir.dt.bfloat16
AF = mybir.ActivationFunctionType
ALU = mybir.AluOpType
AX = mybir.AxisListType

ENG = ["pe", "act", "dve", "pool", "sp"]

D = 2048
KC = 16
S_ALL = 4096
OWN = 1024
HID = 5632
HC = 44
EPS = 1e-6
NEG = -30000.0
C_NAQ, C_NAK, C_NAV, C_DFQ, C_DFK, C_DFV, C_GA, C_GB = 0, 1024, 2048, 3072, 4096, 5120, 6144, 8192
DEBUG = False


class Op:
    __slots__ = ("e", "idx", "fn", "deps", "dma", "flag", "cum")

    def __init__(self, e, idx, fn, deps, dma):
        self.e, self.idx, self.fn, self.deps, self.dma = e, idx, fn, deps, dma
        self.flag = False
        self.cum = 0


class Sched:
    def __init__(self, nc, stack):
        self.nc = nc
        self.esem = {e: stack.enter_context(nc.semaphore("s_" + e)) for e in ENG}
        self.psem = stack.enter_context(nc.semaphore("s_phase"))
        self.stack = stack
        self.dsem = {}
        self.dma_cnt = {}
        self.ecum = {e: 0 for e in ENG}
        self.waited = {e: {} for e in ENG}
        self.phase = 0
        self.nops = 0
        self.nwaits = 0
        self.reset()

    def reset(self):
        self.ops = {e: [] for e in ENG}
        self.buf = {}
        self.phase_dma = set()

    def add(self, e, fn, r=(), w=(), dma=None):
        idx = len(self.ops[e])
        deps = set()
        for k in r:
            st = self.buf.get(k)
            if st is not None and st[0] is not None:
                deps.add(st[0])
        for k in w:
            st = self.buf.get(k)
            if st is not None:
                if st[0] is not None:
                    deps.add(st[0])
                deps.update(st[1].values())
        if dma is not None:
            self.dma_cnt[dma] = self.dma_cnt.get(dma, 0) + 16
            ref = ("dma", dma, self.dma_cnt[dma])
            rkey = ("dma", dma)
            self.phase_dma.add(dma)
        else:
            ref = ("eng", e, idx)
            rkey = ("eng", e)
        if e == "pe":
            deps = {d for d in deps if not (d[0] == "eng" and d[1] == "pe")}
        op = Op(e, idx, fn, deps, dma)
        for k in r:
            st = self.buf.setdefault(k, [None, {}])
            st[1][rkey] = ref
        for k in w:
            self.buf[k] = [ref, {}]
        self.ops[e].append(op)
        self.nops += 1
        return op

    def emit(self):
        nc = self.nc
        for e in ENG:
            for op in self.ops[e]:
                for d in op.deps:
                    if d[0] == "eng":
                        self.ops[d[1]][d[2]].flag = True
            last = None
            for op in self.ops[e]:
                if op.dma is None:
                    last = op
            if last is not None:
                last.flag = True
        for e in ENG:
            c = self.ecum[e]
            for op in self.ops[e]:
                if op.flag and op.dma is None:
                    c += 1
                op.cum = c
            self.ecum[e] = c
        for k in sorted(self.phase_dma, key=str):
            if k not in self.dsem:
                self.dsem[k] = self.stack.enter_context(nc.semaphore("d%d" % len(self.dsem)))
        phase = self.phase

        def run(e, eng):
            waited = self.waited[e]
            if phase > 0:
                eng.wait_ge(self.psem, phase)
            for op in self.ops[e]:
                need = {}
                for d in op.deps:
                    if d[0] == "eng":
                        sem = self.esem[d[1]]
                        val = self.ops[d[1]][d[2]].cum
                        key = ("e", d[1])
                    else:
                        sem = self.dsem[d[1]]
                        val = d[2]
                        key = ("d", d[1])
                    if val > need.get(key, (None, 0))[1]:
                        need[key] = (sem, val)
                for key, (sem, val) in need.items():
                    if waited.get(key, 0) >= val:
                        continue
                    waited[key] = val
                    eng.wait_ge(sem, val)
                    self.nwaits += 1
                ins = op.fn(eng)
                if op.dma is not None:
                    ins.then_inc(self.dsem[op.dma], 16)
                elif op.flag:
                    ins.then_inc(self.esem[e], 1)
            if e == "sp":
                for e2 in ENG:
                    if e2 != "sp" and self.ecum[e2] > 0:
                        eng.wait_ge(self.esem[e2], self.ecum[e2])
                for k in sorted(self.phase_dma, key=str):
                    eng.wait_ge(self.dsem[k], self.dma_cnt[k])
                eng.nop().then_inc(self.psem, 1)

        with nc.Block() as block:
            @block.tensor
            def _(eng):
                run("pe", eng)

            @block.scalar
            def _(eng):
                run("act", eng)

            @block.vector
            def _(eng):
                run("dve", eng)

            @block.gpsimd
            def _(eng):
                run("pool", eng)

            @block.sync
            def _(eng):
                run("sp", eng)
        self.phase += 1
        self.reset()


class Rot:
    def __init__(self, items):
        self.items = list(items)
        self.i = 0

    def next(self):
        v = self.items[self.i % len(self.items)]
        self.i += 1
        return v


def build_program():
    nc = bass.Bass("TRN2", target_bir_lowering=False)
    din = lambda name, shape, dt=F32: nc.dram_tensor(name, list(shape), dt, kind="ExternalInput").ap()
    skind = "ExternalOutput" if DEBUG else "Internal"
    dscr = lambda name, shape, dt: nc.dram_tensor(name, list(shape), dt, kind=skind).ap()

    xr = din("xr", [S_ALL, D])
    pr = din("pr", [OWN, 256])
    w_in = din("w_in", [D, 10240])
    w_na_out = din("w_na_out", [1024, D])
    w_df_out = din("w_df_out", [1024, D])
    w_o = din("w_o", [D, D])
    w_gate = din("w_gate", [D, HID])
    w_up = din("w_up", [D, HID])
    w_down = din("w_down", [HID, D])
    w_pg = din("w_ple_gate", [D, D])
    w_pp = din("w_ple_proj", [256, D])
    gcols_d = din("gcols", [128, 48])
    gvec_d = din("gvec", [128, 8])
    lamv_d = din("lamv", [128, 256])
    ident_d = din("ident", [128, 128])
    rotT_d = din("rotT", [128, 128])
    ct_d = din("ropec", [128, S_ALL])
    st_d = din("ropes", [128, S_ALL])
    t2_d = din("t2", [128, 8 * 15 * 64])
    rm_d = din("rowmask", [128, 16 * 6 * 64])
    out = nc.dram_tensor("out", [OWN, D], F32, kind="ExternalOutput").ap()

    KNA = dscr("KNA", [8, 128, 1536], BF16)
    QNA = dscr("QNA", [8, 128, OWN], BF16)
    VNA = dscr("VNA", [8, 1536, 128], BF16)
    KDF = dscr("KDF", [8, 128, S_ALL], BF16)
    QDF = dscr("QDF", [8, 128, OWN], BF16)
    VDF = dscr("VDF", [8, 128, 32, 128], BF16)
    X1 = dscr("X1", [OWN, D], F32)
    X2 = dscr("X2", [OWN, D], F32)
    if DEBUG:
        OAT = dscr("OAT", [128, 8, OWN], BF16)
        OBT = dscr("OBT", [128, 8, OWN], BF16)
        MT = dscr("MT", [128, 16, OWN], BF16)

    with contextlib.ExitStack() as st0:
        S = Sched(nc, st0)

        def sb(stack, name, shape, dt):
            return stack.enter_context(nc.sbuf_tensor("sb_" + name, list(shape), dt))

        ps_all = st0.enter_context(nc.psum_tensor("ps_all", [128, 4096], F32))
        ps_bf = ps_all.bitcast(BF16)
        PS = [ps_all[:, i * 512:(i + 1) * 512] for i in range(8)]
        PSB = [ps_bf[:, i * 1024:(i + 1) * 1024] for i in range(8)]
        pk = lambda i: ("ps", i)

        ident_f = sb(st0, "ident_f", [128, 128], F32)
        rot_f = sb(st0, "rot_f", [128, 128], F32)
        ident_b = sb(st0, "ident_b", [128, 128], BF16)
        rotT_b = sb(st0, "rotT_b", [128, 128], BF16)
        ones_b = sb(st0, "ones_b", [128, 128], BF16)
        bones_b = sb(st0, "bones_b", [128, 128], BF16)
        ones_f = sb(st0, "ones_f", [128, 128], F32)
        eps_c = sb(st0, "eps_c", [128, 1], F32)
        gcols = sb(st0, "gcols", [128, 48], F32)
        gvec = sb(st0, "gvec", [128, 8], F32)
        gsub08 = sb(st0, "gsub08", [128, 1], F32)
        lamv = sb(st0, "lamv", [128, 256], F32)
        lamw = sb(st0, "lamw", [128, 128], F32)
        lams = sb(st0, "lams", [128, 4], F32)
        nlam = sb(st0, "nlam", [128, 1], F32)

        def dma(e, out_ap, in_ap, r=(), w=(), key=None):
            S.add(e, lambda eng: eng.dma_start(out=out_ap, in_=in_ap), r=r, w=w, dma=key)

        dma("sp", ident_f[:], ident_d[:, :], w=["c"], key="c")
        dma("sp", rot_f[:], rotT_d[:, :], w=["c"], key="c")
        dma("sp", gcols[:], gcols_d[:, :], w=["c"], key="c")
        dma("sp", gvec[:], gvec_d[:, :], w=["c"], key="c")
        dma("sp", lamv[:], lamv_d[:, :], w=["c"], key="c")
        S.add("dve", lambda e: e.memset(ones_f[:], 1.0), w=["ones_f"])
        S.add("dve", lambda e: e.memset(eps_c[:], EPS), w=["eps"])
        S.add("dve", lambda e: e.memset(bones_b[:], 0.0), w=["bones"])
        S.add("dve", lambda e: e.tensor_copy(out=ident_b[:], in_=ident_f[:]), r=["c"], w=["k1"])
        S.add("dve", lambda e: e.tensor_copy(out=rotT_b[:], in_=rot_f[:]), r=["c"], w=["k2"])
        S.add("dve", lambda e: e.tensor_copy(out=ones_b[:], in_=ones_f[:]), r=["ones_f"], w=["k3"])
        S.add("dve", lambda e: e.tensor_copy(out=bones_b[0:64, 0:64], in_=ones_f[0:64, 0:64]), r=["ones_f", "bones"], w=["bones"])
        S.add("dve", lambda e: e.tensor_copy(out=bones_b[64:128, 64:128], in_=ones_f[64:128, 64:128]), r=["ones_f", "bones"], w=["bones"])
        S.add("dve", lambda e: e.tensor_scalar(out=gsub08[:], in0=gvec[:, 4:5], scalar1=0.8, scalar2=None, op0=ALU.mult), r=["c"], w=["k5"])
        S.add("dve", lambda e: e.tensor_tensor(out=lamw[:, 0:64], in0=lamv[:, 0:64], in1=lamv[:, 64:128], op=ALU.mult), r=["c"], w=["lw0"])
        S.add("dve", lambda e: e.tensor_tensor(out=lamw[:, 64:128], in0=lamv[:, 128:192], in1=lamv[:, 192:256], op=ALU.mult), r=["c"], w=["lw1"])
        S.add("dve", lambda e: e.reduce_sum(out=lams[:, 0:1], in_=lamw[:, 0:64], axis=AX.X), r=["lw0"], w=["ls0"])
        S.add("dve", lambda e: e.reduce_sum(out=lams[:, 1:2], in_=lamw[:, 64:128], axis=AX.X), r=["lw1"], w=["ls1"])
        S.add("act", lambda e: e.activation(out=lams[:, 2:4], in_=lams[:, 0:2], func=AF.Exp), r=["ls0", "ls1"], w=["ls2"])
        S.add("dve", lambda e: e.tensor_tensor(out=nlam[:], in0=lams[:, 3:4], in1=lams[:, 2:3], op=ALU.subtract), r=["ls2"], w=["nlam"])
        S.add("dve", lambda e: e.tensor_scalar(out=nlam[:], in0=nlam[:], scalar1=-0.2, scalar2=None, op0=ALU.add), r=["nlam"], w=["nlam"])

        def mm(outp, lhsT, rhs, start, stop, r, w):
            S.add("pe", lambda e: e.matmul(outp, lhsT=lhsT, rhs=rhs, start=start, stop=stop), r=r, w=w)

        def norm_phase(stack, src_tile, ntiles, gbase, dst, hkey, tag, use_pool=False, nxt=3):
            xt = [sb(stack, "%s_xt%d" % (tag, i), [128, D], F32) for i in range(nxt)]
            junk = sb(stack, tag + "_junk", [128, D], BF16)
            xh = [sb(stack, "%s_xh%d" % (tag, i), [128, D], BF16) for i in range(2)]
            ssq = sb(stack, tag + "_ssq", [128, ntiles], F32)
            rst = sb(stack, tag + "_rst", [128, ntiles], F32)
            gt = sb(stack, tag + "_gt", [128, 16, 128], F32)
            for c in range(16):
                S.add("dve", lambda e, c=c: e.tensor_scalar(out=gt[:, c, :], in0=ones_f[:], scalar1=gcols[:, gbase + c:gbase + c + 1],
                                                            scalar2=None, op0=ALU.mult), r=["c", "ones_f"], w=[("gt", c)])
            gtk = [("gt", c) for c in range(16)]
            banks = Rot([(0, 1), (2, 3)])
            NX = len(xt)

            def load(i):
                dma("sp", xt[i % NX][:], src_tile(i), w=[("xt", i % NX)], key=("xt", i % NX))

            def square(i):
                bx = i % NX
                S.add("act", lambda e: e.activation(out=junk[:], in_=xt[bx][:], func=AF.Square, accum_out=ssq[:, i:i + 1]),
                      r=[("xt", bx)], w=["junk", ("ssq", i)])

            for i in range(min(NX - 1, ntiles)):
                load(i)
            square(0)
            for i in range(ntiles):
                bx, b2 = i % NX, i % 2
                if i + NX - 1 < ntiles:
                    load(i + NX - 1)
                S.add("act", lambda e, i=i: e.activation(out=rst[:, i:i + 1], in_=ssq[:, i:i + 1], func=AF.Ln, bias=eps_c[:], scale=1.0 / D),
                      r=[("ssq", i), "eps"], w=[("rst", i)])
                if i + 1 < ntiles:
                    square(i + 1)
                S.add("act", lambda e, i=i: e.activation(out=rst[:, i:i + 1], in_=rst[:, i:i + 1], func=AF.Exp, scale=-0.5), r=[("rst", i)], w=[("rst", i)])
                if use_pool:
                    S.add("pool", lambda e, bx=bx, b2=b2, i=i: e.tensor_scalar(out=xh[b2][:], in0=xt[bx][:], scalar1=rst[:, i:i + 1],
                                                                               scalar2=1.0, op0=ALU.mult, op1=ALU.mult),
                          r=[("xt", bx), ("rst", i)], w=[("xh", b2)])
                else:
                    S.add("act", lambda e, bx=bx, b2=b2, i=i: e.activation(out=xh[b2][:], in_=xt[bx][:], func=AF.Copy, scale=rst[:, i:i + 1]),
                          r=[("xt", bx), ("rst", i)], w=[("xh", b2)])
                ba, bb = banks.next()
                for c in range(16):
                    bk = ba if c < 8 else bb
                    S.add("pe", lambda e, c=c, bk=bk, b2=b2: e.transpose(out=PSB[bk][:, (c % 8) * 128:(c % 8 + 1) * 128],
                                                                          in_=xh[b2][:, c * 128:(c + 1) * 128], identity=ident_b[:]),
                          r=[("xh", b2), "k1"], w=[pk(bk)])
                for hf, bk in ((0, ba), (1, bb)):
                    S.add("dve", lambda e, hf=hf, bk=bk, i=i: e.tensor_tensor(out=dst(hf, i), in0=PSB[bk].rearrange("p (c t) -> p c t", c=8),
                                                                             in1=gt[:, hf * 8:(hf + 1) * 8, :], op=ALU.mult),
                          r=[pk(bk)] + gtk, w=[hkey(i)])

        def wload(dst_ap, w_dram, k0, kn, c0, cn, key):
            src = w_dram[k0:k0 + kn, c0:c0 + cn].rearrange("(c p) n -> p c n", p=128)
            dma("pool", dst_ap, src, w=[key], key=key)

        stA0 = contextlib.ExitStack()
        hT_own = sb(stA0, "hT_own", [128, 16, OWN], BF16)
        with contextlib.ExitStack() as stA:
            hT_oth = sb(stA, "hT_oth", [128, 16, S_ALL - OWN], BF16)

            def hdst(hf, i):
                if i < 8:
                    return hT_own[:, hf * 8:(hf + 1) * 8, i * 128:(i + 1) * 128]
                return hT_oth[:, hf * 8:(hf + 1) * 8, (i - 8) * 128:(i - 7) * 128]

            wb = [sb(stA, "wb%d" % i, [128, 16, 256], BF16) for i in range(2)]
            with contextlib.ExitStack() as stA1:
                dma("pool", wb[0][:], w_in[0:D, C_NAQ:C_NAQ + 256].rearrange("(c p) n -> p c n", p=128), w=[("wb", 0)], key=("wb", 0))
                norm_phase(stA1, lambda i: xr[i * 128:(i + 1) * 128, :], 32, 0, hdst, lambda i: ("h", i), "a1", use_pool=True, nxt=4)
                S.emit()

            def hT(c, t0, n):
                if t0 < OWN:
                    return hT_own[:, c, t0:t0 + n]
                return hT_oth[:, c, t0 - OWN:t0 - OWN + n]

            with contextlib.ExitStack() as stA2:
                Ct = sb(stA2, "Ct", [128, S_ALL], F32)
                St = sb(stA2, "St", [128, S_ALL], F32)
                preloaded = [True]
                sq = [sb(stA2, "sq%d" % i, [128, 512], BF16) for i in range(3)]
                sd = [sb(stA2, "sd%d" % i, [128, 512], F32) for i in range(3)]
                Gb = [sb(stA2, "G%d" % i, [128, 512], BF16) for i in range(3)]
                Ab = [sb(stA2, "A%d" % i, [128, 512], BF16) for i in range(3)]
                Bb = [sb(stA2, "B%d" % i, [128, 512], BF16) for i in range(3)]
                yst = [sb(stA2, "yst%d" % i, [128, 512], BF16) for i in range(3)]
                dma("sp", Ct[:], ct_d[:, :], w=["Ct"], key="Ct")
                dma("sp", St[:], st_d[:, :], w=["St"], key="St")
                accb = Rot([0, 1, 2, 3])
                auxb = Rot([4, 5, 6, 7])
                wrot = Rot([0, 1])
                wk = Rot([0, 1, 2])
                yk = Rot([0, 1, 2])

                def post_norm(pbank, nt, gi, dkn, dest, rope, t0):
                    k2 = wk.next()
                    y = yk.next()
                    st = {}

                    def s0():
                        S.add("act", lambda e: e.activation(out=sq[k2][:, :nt], in_=PS[pbank][:, :nt], func=AF.Square), r=[pk(pbank)], w=[("sq", k2)])
                        if rope:
                            S.add("act", lambda e: e.activation(out=Gb[k2][:, :nt], in_=PS[pbank][:, :nt], func=AF.Copy, scale=gvec[:, gi:gi + 1]),
                                  r=[pk(pbank)], w=[("G", k2)])
                        ax = auxb.next()
                        st["ax"] = ax
                        mm(PS[ax][:, :nt], bones_b[:] if rope else ones_b[:], sq[k2][:, :nt], True, True, r=[("sq", k2)], w=[pk(ax)])
                        if rope:
                            S.add("dve", lambda e: e.tensor_tensor(out=Ab[k2][:, :nt], in0=Gb[k2][:, :nt], in1=Ct[:, t0:t0 + nt], op=ALU.mult),
                                  r=[("G", k2), "Ct"], w=[("A", k2)])
                            S.add("dve", lambda e: e.tensor_tensor(out=Bb[k2][:, :nt], in0=Gb[k2][:, :nt], in1=St[:, t0:t0 + nt], op=ALU.mult),
                                  r=[("G", k2), "St"], w=[("B", k2)])

                    def s1():
                        ax = st["ax"]
                        S.add("act", lambda e: e.activation(out=sd[k2][:, :nt], in_=PS[ax][:, :nt], func=AF.Ln, bias=eps_c[:], scale=1.0 / dkn),
                              r=[pk(ax)], w=[("sd", k2)])
                        S.add("act", lambda e: e.activation(out=sd[k2][:, :nt], in_=sd[k2][:, :nt], func=AF.Exp, scale=-0.5), r=[("sd", k2)], w=[("sd", k2)])
                        if not rope:
                            S.add("dve", lambda e: e.scalar_tensor_tensor(out=yst[y][:, :nt], in0=PS[pbank][:, :nt], scalar=gvec[:, gi:gi + 1],
                                                                          in1=sd[k2][:, :nt], op0=ALU.mult, op1=ALU.mult),
                                  r=[pk(pbank), ("sd", k2)], w=[("yst", y)])
                        else:
                            ax2 = auxb.next()
                            mm(PS[ax2][:, :nt], ident_b[:], Ab[k2][:, :nt], True, False, r=[("A", k2)], w=[pk(ax2)])
                            mm(PS[ax2][:, :nt], rotT_b[:], Bb[k2][:, :nt], False, True, r=[("B", k2)], w=[pk(ax2)])
                            S.add("dve", lambda e: e.tensor_tensor(out=yst[y][:, :nt], in0=PS[ax2][:, :nt], in1=sd[k2][:, :nt], op=ALU.mult),
                                  r=[pk(ax2), ("sd", k2)], w=[("yst", y)])
                        dma("sp", dest, yst[y][:, :nt], r=[("yst", y)], key=("yst", y))

                    return [s0, s1]

                pipe = []

                def pipe_step(new_stages=None):
                    for stages in pipe:
                        if stages:
                            stages.pop(0)()
                    while pipe and not pipe[0]:
                        pipe.pop(0)
                    if new_stages is not None:
                        pipe.append(list(new_stages))

                def pipe_drain():
                    while pipe:
                        pipe_step(None)

                def hkeys(t0, n):
                    return [("h", i) for i in range(t0 // 128, (t0 + n + 127) // 128)]

                def fm_proj(colbase, blocks, gi, dkn, destf, rope):
                    for b256 in range(4):
                        wi = wrot.next()
                        if preloaded[0]:
                            preloaded[0] = False
                        else:
                            wload(wb[wi][:], w_in, 0, D, colbase + b256 * 256, 256, ("wb", wi))
                        for half in range(2):
                            n = b256 * 2 + half
                            for (t0, nt, d0) in blocks:
                                bank = accb.next()
                                for c in range(KC):
                                    mm(PS[bank][:, :nt], wb[wi][:, c, half * 128:(half + 1) * 128], hT(c, t0, nt), c == 0, c == KC - 1,
                                       r=[("wb", wi)] + hkeys(t0, nt), w=[pk(bank)])
                                pipe_step(post_norm(bank, nt, gi, dkn, destf(n, d0, nt), rope, t0))

                vst = [sb(stA2, "vst%d" % i, [128, 512], BF16) for i in range(2)]
                vk = Rot([0, 1])

                def tm_proj(colbase, tiles, destf):
                    for b256 in range(4):
                        wi = wrot.next()
                        wload(wb[wi][:], w_in, 0, D, colbase + b256 * 256, 256, ("wb", wi))
                        for p0 in range(0, len(tiles), 2):
                            bank = accb.next()
                            for tt in range(2):
                                t0 = tiles[p0 + tt]
                                for c in range(KC):
                                    mm(PS[bank][:, tt * 256:(tt + 1) * 256], hT(c, t0, 128), wb[wi][:, c, :], c == 0, c == KC - 1,
                                       r=[("wb", wi)] + hkeys(t0, 128), w=[pk(bank)])
                            v = vk.next()
                            S.add("act", lambda e, bank=bank, v=v: e.copy(out=vst[v][:], in_=PS[bank]), r=[pk(bank)], w=[("vst", v)])
                            for tt in range(2):
                                dma("sp", destf(p0 + tt, b256), vst[v][:, tt * 256:(tt + 1) * 256].rearrange("p (h d) -> p h d", h=2),
                                    r=[("vst", v)], key=("vst", v))

                na_blocks = [(S_ALL - 256, 256, 0), (0, 512, 256), (512, 512, 768), (1024, 256, 1280)]
                own_blocks = [(0, 512, 0), (512, 512, 512)]
                all_blocks = [(t * 512, 512, t * 512) for t in range(8)]
                fm_proj(C_NAQ, own_blocks, 0, 128, lambda n, d0, nt: QNA[n, :, d0:d0 + nt], False)
                fm_proj(C_NAK, na_blocks, 1, 128, lambda n, d0, nt: KNA[n, :, d0:d0 + nt], False)
                na_tiles = [S_ALL - 256, S_ALL - 128] + [i * 128 for i in range(10)]
                pipe_drain()
                tm_proj(C_NAV, na_tiles, lambda m, b: VNA[2 * b:2 * b + 2, m * 128:(m + 1) * 128, :].rearrange("h p d -> p h d"))
                fm_proj(C_DFQ, own_blocks, 2, 64, lambda n, d0, nt: QDF[n, :, d0:d0 + nt], True)
                fm_proj(C_DFK, all_blocks, 3, 64, lambda n, d0, nt: KDF[n, :, d0:d0 + nt], True)
                pipe_drain()
                tm_proj(C_DFV, [i * 128 for i in range(32)], lambda m, b: VDF[2 * b:2 * b + 2, :, m, :].rearrange("h p d -> p h d"))
                S.emit()

        stBC = contextlib.ExitStack()
        oaT = sb(stBC, "oaT", [128, 8, OWN], BF16)
        obT = sb(stBC, "obT", [128, 8, OWN], BF16)

        with contextlib.ExitStack() as stB:
            T2 = sb(stB, "T2", [128, 8, 15, 64], F32)
            rmk = sb(stB, "rmk", [128, 16, 6, 64], F32)
            kT = [sb(stB, "nkT%d" % i, [128, 1536], BF16) for i in range(2)]
            qT = [sb(stB, "nqT%d" % i, [128, OWN], BF16) for i in range(2)]
            Ve = [sb(stB, "nVe%d" % i, [128, 12, 128], BF16) for i in range(2)]
            Vo = [sb(stB, "nVo%d" % i, [128, 11, 128], BF16) for i in range(2)]
            sbf = [sb(stB, "nsb%d" % i, [128, 6, 64], F32) for i in range(4)]
            Pn = [sb(stB, "nP%d" % i, [128, 6, 64], BF16) for i in range(4)]
            rcn = [sb(stB, "nrc%d" % i, [128, 512], F32) for i in range(2)]
            dma("sp", T2[:, 0, :, :].rearrange("p d c -> p (d c)"), t2_d[:, 0:960], w=["T2a"], key="T2a")
            dma("sp", rmk[:, 0:4, :, :].rearrange("p r c q -> p (r c q)"), rm_d[:, 0:4 * 384], w=["rmka"], key="rmka")
            sbank = Rot([0, 1, 2, 3])
            obank = Rot([(4, 5), (6, 7)])
            sk = Rot([0, 1, 2, 3])
            na_scale = 128.0 ** -0.5
            pipe = []

            def na_unit(h, hb, half, rr, bo, bz):
                rl = half * 8 + rr
                if rl <= 3:
                    vlo, vhi = rl, 11
                elif rl <= 12:
                    vlo, vhi = rl, rl + 7
                else:
                    vlo, vhi = 12, rl + 7
                v0s = list(range(vlo, vhi + 1, 2))
                n = len(v0s)
                drp0 = vlo - rl + 3
                bs = sbank.next()
                k2 = sk.next()
                for ci, v0 in enumerate(v0s):
                    mm(PS[bs][:, ci * 64:(ci + 1) * 64], kT[hb][:, v0 * 64:v0 * 64 + 128], qT[hb][:, rl * 64:(rl + 1) * 64], True, True,
                       r=[("kT", hb), ("qT", hb)], w=[pk(bs)])
                S.add("dve", lambda e: e.scalar_tensor_tensor(
                    out=sbf[k2][:, 0:n, :], in0=PS[bs][:, 0:n * 64].rearrange("p (n c) -> p n c", n=n), scalar=na_scale,
                    in1=T2[:, h, drp0:drp0 + 2 * n - 1:2, :], op0=ALU.mult, op1=ALU.add),
                    r=[pk(bs), "T2a" if h == 0 else "T2b"], w=[("sbf", k2)])
                if not (4 <= rl <= 12):
                    S.add("dve", lambda e: e.tensor_tensor(out=sbf[k2][:, 0:n, :], in0=sbf[k2][:, 0:n, :], in1=rmk[:, rl, 0:n, :], op=ALU.add),
                          r=[("sbf", k2), "rmka" if rl < 4 else "rmkb"], w=[("sbf", k2)])
                S.add("act", lambda e: e.activation(out=Pn[k2][:, 0:n, :], in_=sbf[k2][:, 0:n, :], func=AF.Exp),
                      r=[("sbf", k2)], w=[("Pn", k2)])

                def back():
                    for ci, v0 in enumerate(v0s):
                        vt = Ve[hb][:, v0 // 2, :] if v0 % 2 == 0 else Vo[hb][:, (v0 - 1) // 2, :]
                        mm(PS[bo][:, rr * 64:(rr + 1) * 64], vt, Pn[k2][:, ci, :], ci == 0, ci == n - 1,
                           r=[("Pn", k2), ("Ve", hb), ("Vo", hb)], w=[pk(bo)])
                    for ci in range(n):
                        mm(PS[bz][:, rr * 64:(rr + 1) * 64], ones_b[:], Pn[k2][:, ci, :], ci == 0, ci == n - 1,
                           r=[("Pn", k2)], w=[pk(bz)])
                    if rr == 7:
                        r2 = (h * 2 + half) % 2

                        def fin():
                            S.add("act", lambda e: e.activation(out=rcn[r2][:], in_=PS[bz], func=AF.Ln), r=[pk(bz)], w=[("rcn", r2)])
                            S.add("act", lambda e: e.activation(out=rcn[r2][:], in_=rcn[r2][:], func=AF.Exp, scale=-1.0), r=[("rcn", r2)], w=[("rcn", r2)])
                            S.add("dve", lambda e: e.tensor_tensor(out=oaT[:, h, half * 512:(half + 1) * 512], in0=PS[bo],
                                                                   in1=rcn[r2][:], op=ALU.mult),
                                  r=[pk(bo), ("rcn", r2)], w=[("oaT", h, half)])
                        fins.append([4, fin])
                return back

            LAG = 3
            fins = []
            for h in range(8):
                hb = h % 2
                dma("sp", kT[hb][:], KNA[h, :, :], w=[("kT", hb)], key=("kT", hb))
                dma("sp", qT[hb][:], QNA[h, :, :], w=[("qT", hb)], key=("qT", hb))
                dma("sp", Ve[hb][:], VNA[h, :, :].rearrange("(m p) d -> p m d", p=128), w=[("Ve", hb)], key=("Ve", hb))
                dma("sp", Vo[hb][:], VNA[h, 64:64 + 11 * 128, :].rearrange("(m p) d -> p m d", p=128), w=[("Vo", hb)], key=("Vo", hb))
                if h == 0:
                    dma("sp", rmk[:, 4:16, :, :].rearrange("p r c q -> p (r c q)"), rm_d[:, 4 * 384:], w=["rmkb"], key="rmkb")
                    dma("sp", T2[:, 1:8, :, :].rearrange("p h d c -> p (h d c)"), t2_d[:, 960:], w=["T2b"], key="T2b")
                for half in range(2):
                    bo, bz = obank.next()
                    for rr in range(8):
                        pipe.append(na_unit(h, hb, half, rr, bo, bz))
                        if len(pipe) > LAG:
                            pipe.pop(0)()
                        for f in fins:
                            f[0] -= 1
                        while fins and fins[0][0] <= 0:
                            fins.pop(0)[1]()
            while pipe:
                pipe.pop(0)()
            while fins:
                fins.pop(0)[1]()
            if DEBUG:
                dma("sp", OAT[:, :, :], oaT[:], r=[("oaT", h, hf) for h in range(8) for hf in range(2)], key="dbg")
            S.emit()

        with contextlib.ExitStack() as stC:
            kT = [sb(stC, "dkT%d" % i, [128, S_ALL], BF16) for i in range(2)]
            qz0 = [sb(stC, "dqz0%d" % i, [128, OWN], BF16) for i in range(2)]
            qz1 = [sb(stC, "dqz1%d" % i, [128, OWN], BF16) for i in range(2)]
            for i in range(2):
                S.add("dve", lambda e, i=i: e.memset(qz0[i][64:128, :], 0.0), w=[("qz0z", i)])
                S.add("dve", lambda e, i=i: e.memset(qz1[i][0:64, :], 0.0), w=[("qz1z", i)])
            Vd = [sb(stC, "dV%d" % i, [128, 32, 128], BF16) for i in range(2)]
            Pd = [sb(stC, "dP%d" % i, [128, 1024], BF16) for i in range(4)]
            r0 = sb(stC, "dr0", [128, 512], F32)
            r1 = sb(stC, "dr1", [128, 512], F32)
            t0b = sb(stC, "dt0", [128, 512], F32)
            a0s = sb(stC, "da0s", [128, 512], BF16)
            t1b = sb(stC, "dt1", [128, 512], F32)
            dd = sb(stC, "ddd", [128, 512], F32)
            sqd = sb(stC, "dsq", [128, 512], BF16)
            sdd = sb(stC, "dsd", [128, 512], F32)
            df_scale = 64.0 ** -0.5
            sbank = Rot([(0, 1), (2, 3)])
            pkk = Rot([0, 1, 2, 3])
            BO0, BO1, BZ0, BZ1 = 4, 5, 6, 7
            for h in range(8):
                hb = h % 2
                dma("sp", kT[hb][:], KDF[h, :, :], w=[("kT", hb)], key=("kT", hb))
                dma("sp", qz0[hb][0:64, :], QDF[h, 0:64, :], w=[("qz0", hb)], key=("qz0", hb))
                dma("sp", qz1[hb][64:128, :], QDF[h, 64:128, :], w=[("qz1", hb)], key=("qz1", hb))
                dma("sp", Vd[hb][:], VDF[h, :, :, :], w=[("Vd", hb)], key=("Vd", hb))
                for qb in range(2):
                    qs = slice(qb * 512, (qb + 1) * 512)

                    def qk(kc):
                        b0, b1 = sbank.next()
                        mm(PS[b0], kT[hb][:, kc * 128:(kc + 1) * 128], qz0[hb][:, qs], True, True,
                           r=[("kT", hb), ("qz0", hb), ("qz0z", hb)], w=[pk(b0)])
                        mm(PS[b1], kT[hb][:, kc * 128:(kc + 1) * 128], qz1[hb][:, qs], True, True,
                           r=[("kT", hb), ("qz1", hb), ("qz1z", hb)], w=[pk(b1)])
                        p = pkk.next()
                        S.add("act", lambda e, b0=b0, p=p: e.activation(out=Pd[p][:], in_=ps_all[:, b0 * 512:(b0 + 2) * 512], func=AF.Exp, scale=df_scale),
                              r=[pk(b0), pk(b1)], w=[("Pd", p)])
                        return p

                    def av(kc, p):
                        st_, sp_ = kc == 0, kc == 31
                        mm(PS[BO0], Vd[hb][:, kc, :], Pd[p][:, 0:512], st_, sp_, r=[("Pd", p), ("Vd", hb)], w=[pk(BO0)])
                        mm(PS[BO1], Vd[hb][:, kc, :], Pd[p][:, 512:1024], st_, sp_, r=[("Pd", p), ("Vd", hb)], w=[pk(BO1)])
                        mm(PS[BZ1], ones_b[:], Pd[p][:, 512:1024], st_, sp_, r=[("Pd", p)], w=[pk(BZ1)])
                        if st_:
                            S.add("dve", lambda e, p=p: e.tensor_copy(out=PS[BZ0], in_=Pd[p][:, 0:512]), r=[("Pd", p)], w=[pk(BZ0)])
                        else:
                            S.add("dve", lambda e, p=p: e.tensor_tensor(out=PS[BZ0], in0=PS[BZ0], in1=Pd[p][:, 0:512], op=ALU.add),
                                  r=[("Pd", p), pk(BZ0)], w=[pk(BZ0)])

                    pq = [qk(0), qk(1)]
                    for kc in range(32):
                        if kc + 2 < 32:
                            pq.append(qk(kc + 2))
                        av(kc, pq.pop(0))
                    S.add("dve", lambda e: e.tensor_copy(out=a0s[:], in_=PS[BZ0]), r=[pk(BZ0)], w=["a0s"])
                    mm(PS[BZ0], ones_b[:], a0s[:], True, True, r=["a0s"], w=[pk(BZ0)])
                    S.add("act", lambda e: e.activation(out=r1[:], in_=PS[BZ1], func=AF.Ln), r=[pk(BZ1)], w=["r1"])
                    S.add("act", lambda e: e.activation(out=r1[:], in_=r1[:], func=AF.Exp, scale=-1.0), r=["r1"], w=["r1"])
                    S.add("act", lambda e: e.activation(out=r0[:], in_=PS[BZ0], func=AF.Ln), r=[pk(BZ0)], w=["r0"])
                    S.add("act", lambda e: e.activation(out=r0[:], in_=r0[:], func=AF.Exp, scale=-1.0), r=["r0"], w=["r0"])
                    S.add("dve", lambda e: e.tensor_tensor(out=t1b[:], in0=PS[BO1], in1=r1[:], op=ALU.mult), r=[pk(BO1), "r1"], w=["t1"])
                    S.add("dve", lambda e: e.tensor_tensor(out=t0b[:], in0=PS[BO0], in1=r0[:], op=ALU.mult), r=[pk(BO0), "r0"], w=["t0"])
                    S.add("dve", lambda e: e.scalar_tensor_tensor(out=dd[:], in0=t1b[:], scalar=nlam[:, 0:1], in1=t0b[:], op0=ALU.mult, op1=ALU.add),
                          r=["t0", "t1"], w=["dd"])
                    S.add("act", lambda e: e.activation(out=sqd[:], in_=dd[:], func=AF.Square), r=["dd"], w=["sqd"])
                    mm(PS[BZ1], ones_b[:], sqd[:], True, True, r=["sqd"], w=[pk(BZ1)])
                    S.add("act", lambda e: e.activation(out=sdd[:], in_=PS[BZ1], func=AF.Ln, bias=eps_c[:], scale=1.0 / 128), r=[pk(BZ1)], w=["sdd"])
                    S.add("act", lambda e: e.activation(out=sdd[:], in_=sdd[:], func=AF.Exp, scale=-0.5), r=["sdd"], w=["sdd"])
                    S.add("dve", lambda e, h=h, qs=qs: e.scalar_tensor_tensor(out=obT[:, h, qs], in0=dd[:], scalar=gsub08[:, 0:1], in1=sdd[:],
                                                                           op0=ALU.mult, op1=ALU.mult),
                          r=["dd", "sdd"], w=[("obT", h, qb)])
            if DEBUG:
                dma("sp", OBT[:, :, :], obT[:], r=[("obT", h, hf) for h in range(8) for hf in range(2)], key="dbg")
            S.emit()

        def tm_residual(stack, tag, srcs, res_dram, dst_dram, gate=None, akeys=None, mid=None):
            wbufs = []
            for si, (aT, nk, wd) in enumerate(srcs):
                wbufs.append([sb(stack, "%s_w%d_%d" % (tag, si, i), [128, nk, 512], BF16) for i in range(2)])
            res = [sb(stack, "%s_res%d" % (tag, i), [128, 512], F32) for i in range(3)]
            ost = [sb(stack, "%s_ost%d" % (tag, i), [128, 512], F32) for i in range(3)]
            sg = [sb(stack, "%s_sg%d" % (tag, i), [128, 512], F32) for i in range(2)] if gate else None
            bankr = Rot([(0, 1), (2, 3), (4, 5), (6, 7)]) if gate else Rot([(i,) for i in range(8)])
            def res_load(u):
                nb_, tt_ = divmod(u, 8)
                k3_ = u % 3
                dma("sp", res[k3_][:], res_dram[tt_ * 128:(tt_ + 1) * 128, nb_ * 512:(nb_ + 1) * 512], w=[(tag + "res", k3_)], key=(tag + "res", k3_))

            for nb_ in range(2):
                for si, (aT, nk, wd) in enumerate(srcs):
                    wload(wbufs[si][nb_][:], wd, 0, nk * 128, nb_ * 512, 512, (tag + "w%d" % si, nb_))
            if mid is not None:
                mid()
            res_load(0)
            res_load(1)
            for nb in range(4):
                wi = nb % 2
                cs = slice(nb * 512, (nb + 1) * 512)
                if nb > 1:
                    for si, (aT, nk, wd) in enumerate(srcs):
                        wload(wbufs[si][wi][:], wd, 0, nk * 128, nb * 512, 512, (tag + "w%d" % si, wi))
                for tt in range(8):
                    tsl = slice(tt * 128, (tt + 1) * 128)
                    banks = bankr.next()
                    u = nb * 8 + tt
                    k3 = u % 3
                    if u + 2 < 32:
                        res_load(u + 2)
                    for si, (aT, nk, wd) in enumerate(srcs):
                        ak = list(akeys[si](tt)) if akeys else []
                        for c in range(nk):
                            mm(PS[banks[si]], aT[:, c, tsl], wbufs[si][wi][:, c, :], c == 0, c == nk - 1, r=[(tag + "w%d" % si, wi)] + ak, w=[pk(banks[si])])
                    if gate:
                        g2 = k3 % 2
                        S.add("act", lambda e, b=banks[0], g2=g2: e.activation(out=sg[g2][:], in_=PS[b], func=AF.Sigmoid), r=[pk(banks[0])], w=[(tag + "sg", g2)])
                        S.add("dve", lambda e, b=banks[1], g2=g2: e.tensor_tensor(out=sg[g2][:], in0=PS[b], in1=sg[g2][:], op=ALU.mult),
                              r=[pk(banks[1]), (tag + "sg", g2)], w=[(tag + "sg", g2)])
                        S.add("dve", lambda e, g2=g2, k3=k3: e.tensor_tensor(out=ost[k3][:], in0=sg[g2][:], in1=res[k3][:], op=ALU.add),
                              r=[(tag + "sg", g2), (tag + "res", k3)], w=[(tag + "ost", k3)])
                    else:
                        S.add("dve", lambda e, b=banks[0], k3=k3: e.tensor_tensor(out=ost[k3][:], in0=PS[b], in1=res[k3][:], op=ALU.add),
                              r=[pk(banks[0]), (tag + "res", k3)], w=[(tag + "ost", k3)])
                    dma("sp", dst_dram[tsl, cs], ost[k3][:], r=[(tag + "ost", k3)], key=(tag + "ost", k3))

        stD = contextlib.ExitStack()
        mT = sb(stD, "mT", [128, 16, OWN], BF16)
        with contextlib.ExitStack() as stD1:
            wna = [sb(stD1, "wna%d" % i, [128, 8, 128], BF16) for i in range(2)]
            wdf = [sb(stD1, "wdf%d" % i, [128, 8, 128], BF16) for i in range(2)]
            wga = [sb(stD1, "wga%d" % i, [128, 16, 128], BF16) for i in range(2)]
            wgb = [sb(stD1, "wgb%d" % i, [128, 16, 128], BF16) for i in range(2)]
            sga = [sb(stD1, "sga%d" % i, [128, 512], F32) for i in range(2)]
            sgb = [sb(stD1, "sgb%d" % i, [128, 512], F32) for i in range(2)]
            ta = [sb(stD1, "ta%d" % i, [128, 512], F32) for i in range(2)]
            tb = [sb(stD1, "tb%d" % i, [128, 512], F32) for i in range(2)]
            bankr = Rot([(0, 1, 2, 3), (4, 5, 6, 7)])
            kk = Rot([0, 1])
            for b128 in range(16):
                wi = b128 % 2
                c0 = b128 * 128
                wload(wna[wi][:], w_na_out, 0, 1024, c0, 128, ("wna", wi))
                wload(wdf[wi][:], w_df_out, 0, 1024, c0, 128, ("wdf", wi))
                wload(wga[wi][:], w_in, 0, D, C_GA + c0, 128, ("wga", wi))
                wload(wgb[wi][:], w_in, 0, D, C_GB + c0, 128, ("wgb", wi))
                for half in range(1):
                    n = b128
                    hs = slice(0, 128)
                    for tb_ in range(2):
                        ts_ = slice(tb_ * 512, (tb_ + 1) * 512)
                        bya, byb, bga, bgb = bankr.next()
                        for c in range(8):
                            mm(PS[bya], wna[wi][:, c, hs], oaT[:, c, ts_], c == 0, c == 7, r=[("wna", wi)], w=[pk(bya)])
                        for c in range(8):
                            mm(PS[byb], wdf[wi][:, c, hs], obT[:, c, ts_], c == 0, c == 7, r=[("wdf", wi)], w=[pk(byb)])
                        for c in range(16):
                            mm(PS[bga], wga[wi][:, c, hs], hT_own[:, c, ts_], c == 0, c == 15, r=[("wga", wi)], w=[pk(bga)])
                        for c in range(16):
                            mm(PS[bgb], wgb[wi][:, c, hs], hT_own[:, c, ts_], c == 0, c == 15, r=[("wgb", wi)], w=[pk(bgb)])
                        k2 = kk.next()
                        S.add("act", lambda e, bga=bga, k2=k2: e.activation(out=sga[k2][:], in_=PS[bga], func=AF.Sigmoid), r=[pk(bga)], w=[("sga", k2)])
                        S.add("act", lambda e, bgb=bgb, k2=k2: e.activation(out=sgb[k2][:], in_=PS[bgb], func=AF.Sigmoid), r=[pk(bgb)], w=[("sgb", k2)])
                        S.add("dve", lambda e, bya=bya, k2=k2: e.tensor_tensor(out=ta[k2][:], in0=PS[bya], in1=sga[k2][:], op=ALU.mult),
                              r=[pk(bya), ("sga", k2)], w=[("ta", k2)])
                        S.add("dve", lambda e, byb=byb, k2=k2: e.tensor_tensor(out=tb[k2][:], in0=PS[byb], in1=sgb[k2][:], op=ALU.mult),
                              r=[pk(byb), ("sgb", k2)], w=[("tb", k2)])
                        S.add("dve", lambda e, k2=k2, n=n, ts_=ts_: e.tensor_tensor(out=mT[:, n, ts_], in0=ta[k2][:], in1=tb[k2][:], op=ALU.add),
                              r=[("ta", k2), ("tb", k2)], w=[("mT", n, ts_.start)])
            if DEBUG:
                dma("sp", MT[:, :, :], mT[:], r=[("mT", n, t) for n in range(16) for t in (0, 512)], key="dbg")
            tm_residual(stD1, "d2", [(mT, 16, w_o)], xr[0:OWN, :], X1, akeys=[lambda tt: [("mT", n, (tt // 4) * 512) for n in range(16)]])
            S.emit()

        stD.close()
        stBC.close()
        stA0.close()

        with contextlib.ExitStack() as stE:
            actT = sb(stE, "actT", [128, HC, OWN], BF16)
            with contextlib.ExitStack() as stE12:
                hfT = sb(stE12, "hfT", [128, 16, OWN], BF16)
                with contextlib.ExitStack() as stE1:
                    stE2 = stE1
                    wg = [sb(stE2, "wg%d" % i, [128, 16, 256], BF16) for i in range(2)]
                    wu = [sb(stE2, "wu%d" % i, [128, 16, 256], BF16) for i in range(2)]
                    for pb_ in range(2):
                        wload(wg[pb_][:], w_gate, 0, D, pb_ * 256, 256, ("wg", pb_))
                        wload(wu[pb_][:], w_up, 0, D, pb_ * 256, 256, ("wu", pb_))
                    norm_phase(stE1, lambda i: X1[i * 128:(i + 1) * 128, :], 8, 16,
                               lambda hf, i: hfT[:, hf * 8:(hf + 1) * 8, i * 128:(i + 1) * 128], lambda i: ("h", i), "e1", use_pool=True)
                    sl = [sb(stE2, "sl%d" % i, [128, 512], F32) for i in range(2)]
                    bankr = Rot([(0, 1), (2, 3), (4, 5), (6, 7)])
                    kk = Rot([0, 1])
                    for b256 in range(22):
                        wi = b256 % 2
                        if b256 > 1:
                            wload(wg[wi][:], w_gate, 0, D, b256 * 256, 256, ("wg", wi))
                            wload(wu[wi][:], w_up, 0, D, b256 * 256, 256, ("wu", wi))
                        for half in range(2):
                            n = b256 * 2 + half
                            hs = slice(half * 128, (half + 1) * 128)
                            for tb_ in range(2):
                                ts_ = slice(tb_ * 512, (tb_ + 1) * 512)
                                bg, bu = bankr.next()
                                hk_ = [("h", i) for i in range(tb_ * 4, tb_ * 4 + 4)]
                                for c in range(16):
                                    mm(PS[bg], wg[wi][:, c, hs], hfT[:, c, ts_], c == 0, c == 15, r=[("wg", wi)] + hk_, w=[pk(bg)])
                                for c in range(16):
                                    mm(PS[bu], wu[wi][:, c, hs], hfT[:, c, ts_], c == 0, c == 15, r=[("wu", wi)] + hk_, w=[pk(bu)])
                                k2 = kk.next()
                                S.add("act", lambda e, bg=bg, k2=k2: e.activation(out=sl[k2][:], in_=PS[bg], func=AF.Silu), r=[pk(bg)], w=[("sl", k2)])
                                S.add("dve", lambda e, bu=bu, k2=k2, n=n, ts_=ts_: e.tensor_tensor(out=actT[:, n, ts_], in0=PS[bu], in1=sl[k2][:], op=ALU.mult),
                                      r=[pk(bu), ("sl", k2)], w=[("actT", n, ts_.start)])
                    S.emit()
            with contextlib.ExitStack() as stE3:
                wd_ = [sb(stE3, "wd%d" % i, [128, 4, 512], BF16) for i in range(3)]
                res = [sb(stE3, "e3res%d" % i, [128, 512], F32) for i in range(8)]
                ost = [sb(stE3, "e3ost%d" % i, [128, 512], F32) for i in range(3)]
                wr = Rot([0, 1, 2])
                rk = Rot([0, 1, 2])
                for nb in range(4):
                    cs = slice(nb * 512, (nb + 1) * 512)
                    for tt in range(8):
                        dma("sp", res[tt][:], X1[tt * 128:(tt + 1) * 128, cs], w=[("e3res", tt)], key=("e3res", tt))
                    for kg in range(11):
                        wi = wr.next()
                        wload(wd_[wi][:], w_down, kg * 512, 512, nb * 512, 512, ("wd", wi))
                        if kg in (0, 10):
                            order = [(kq, tt) for tt in range(8) for kq in range(4)]
                        else:
                            order = [(kq, tt) for kq in range(4) for tt in range(8)]
                        for kq, tt in order:
                            kc = kg * 4 + kq
                            mm(PS[tt], actT[:, kc, tt * 128:(tt + 1) * 128], wd_[wi][:, kq, :], kc == 0, kc == HC - 1, r=[("wd", wi)], w=[pk(tt)])
                    for tt in range(8):
                        tsl = slice(tt * 128, (tt + 1) * 128)
                        k3 = rk.next()
                        S.add("dve", lambda e, tt=tt, k3=k3: e.tensor_tensor(out=ost[k3][:], in0=PS[tt], in1=res[tt][:], op=ALU.add),
                              r=[pk(tt), ("e3res", tt)], w=[("e3ost", k3)])
                        dma("sp", X2[tsl, cs], ost[k3][:], r=[("e3ost", k3)], key=("e3ost", k3))
                S.emit()

        with contextlib.ExitStack() as stF:
            hpT = sb(stF, "hpT", [128, 16, OWN], BF16)
            pT = sb(stF, "pT", [128, 2, OWN], BF16)
            with contextlib.ExitStack() as stF1:
                pt = [sb(stF1, "pt%d" % i, [128, 256], F32) for i in range(2)]
                pb = [sb(stF1, "pb%d" % i, [128, 256], BF16) for i in range(2)]

                def f_mid():
                    norm_phase(stF1, lambda i: X2[i * 128:(i + 1) * 128, :], 8, 32,
                               lambda hf, i: hpT[:, hf * 8:(hf + 1) * 8, i * 128:(i + 1) * 128], lambda i: ("h", i), "f1", use_pool=True)
                    for i in range(8):
                        b2 = i % 2
                        dma("sp", pt[b2][:], pr[i * 128:(i + 1) * 128, :], w=[("pt", b2)], key=("pt", b2))
                        S.add("act", lambda e, b2=b2: e.copy(out=pb[b2][:], in_=pt[b2][:]), r=[("pt", b2)], w=[("pb", b2)])
                        bk = 4 + b2
                        for c in range(2):
                            S.add("pe", lambda e, c=c, bk=bk, b2=b2: e.transpose(out=PSB[bk][:, c * 128:(c + 1) * 128], in_=pb[b2][:, c * 128:(c + 1) * 128],
                                                                                  identity=ident_b[:]), r=[("pb", b2)], w=[pk(bk)])
                        S.add("dve", lambda e, bk=bk, i=i: e.tensor_copy(out=pT[:, :, i * 128:(i + 1) * 128],
                                                                         in_=PSB[bk][:, 0:256].rearrange("p (c t) -> p c t", c=2)),
                              r=[pk(bk)], w=[("pT", i)])

                tm_residual(stF1, "f2", [(hpT, 16, w_pg), (pT, 2, w_pp)], X2, out, gate=True,
                            akeys=[lambda tt: [("h", tt)], lambda tt: [("pT", tt)]], mid=f_mid)
                S.emit()
        print("bass program: ops=%d waits=%d dma_sems=%d" % (S.nops, S.nwaits, len(S.dsem)))
    return nc


_NC_CACHE = {}


def _host_consts():
    ident = np.eye(128, dtype=np.float32)
    R = np.zeros((128, 128), np.float32)
    for p in range(128):
        if p % 64 < 32:
            R[p, p + 32] = -1.0
        else:
            R[p, p - 32] = 1.0
    rotT = np.ascontiguousarray(R.T)
    inv = (1.0 / (10000.0 ** (np.arange(0, 64, 2, dtype=np.float32) / np.float32(64)))).astype(np.float32)
    ang = np.arange(S_ALL, dtype=np.float32)[:, None] * inv[None, :]
    cos = np.cos(ang).astype(np.float32)
    sin = np.sin(ang).astype(np.float32)
    return ident, rotT, cos, sin


def _t2_table(rpb):
    c = np.arange(64)
    cs = np.clip(c - 8, 0, 48)
    kc = np.arange(64)
    inwin = (kc[:, None] >= cs[None, :]) & (kc[:, None] < cs[None, :] + 16)
    dc = np.clip(kc[:, None] - c[None, :] + 15, 0, 30)
    t2 = np.full((2, 64, 8, 15, 64), NEG, np.float32)
    for a in range(2):
        for drp in range(15):
            dr = drp + a
            if dr > 14:
                continue
            vals = rpb[:, dr, :][:, dc]
            vals = np.where(inwin[None], vals, np.float32(NEG))
            t2[a, :, :, drp, :] = vals.transpose(1, 0, 2)
    return np.ascontiguousarray(t2.reshape(128, 8 * 15 * 64))


def _rowmask(j):
    rm = np.zeros((2, 64, 16, 6), np.float32)
    for rl in range(16):
        r = 16 * j + rl
        rs = min(max(r - 4, 0), 56)
        if rl <= 3:
            vlo, vhi = rl, 11
        elif rl <= 12:
            vlo, vhi = rl, rl + 7
        else:
            vlo, vhi = 12, rl + 7
        for ci, v0 in enumerate(range(vlo, vhi + 1, 2)):
            for a in range(2):
                rho = 16 * j - 4 + v0 + a
                ok = rs <= rho <= rs + 7
                rm[a, :, rl, ci] = 0.0 if ok else NEG
    return np.ascontiguousarray(np.broadcast_to(rm.reshape(128, 96, 1), (128, 96, 64)).reshape(128, 96 * 64))


def kernel(x, p, g_mix, w_in, g_na_q, g_na_k, na_rpb, g_df_q, g_df_k, lam_q1, lam_k1, lam_q2, lam_k2,
           g_df_sub, w_na_out, w_df_out, w_o, g_ffn, w_gate, w_up, w_down, g_ple, w_ple_gate, w_ple_proj):
    f = lambda a: np.ascontiguousarray(np.asarray(a, dtype=np.float32))
    x = f(x); p = f(p)
    if "nc" not in _NC_CACHE:
        _NC_CACHE["nc"] = build_program()
    nc = _NC_CACHE["nc"]
    ident, rotT, cos, sin = _host_consts()
    col = lambda g: f(g).reshape(16, 128).T
    gcols = np.ascontiguousarray(np.concatenate([col(g_mix[0]), col(g_ffn[0]), col(g_ple[0])], axis=1))
    gvec = np.zeros((128, 8), np.float32)
    gvec[:, 0] = f(g_na_q[0]); gvec[:, 1] = f(g_na_k[0])
    gvec[:, 2] = np.tile(f(g_df_q[0]), 2); gvec[:, 3] = np.tile(f(g_df_k[0]), 2)
    gvec[:, 4] = f(g_df_sub[0])
    lamv = np.ascontiguousarray(np.broadcast_to(np.concatenate([f(lam_q1[0]), f(lam_k1[0]), f(lam_q2[0]), f(lam_k2[0])])[None, :], (128, 256)))
    t2 = _t2_table(f(na_rpb[0]))
    shared = {
        "w_in": f(w_in[0]), "w_na_out": f(w_na_out[0]), "w_df_out": f(w_df_out[0]), "w_o": f(w_o[0]),
        "w_gate": f(w_gate[0]), "w_up": f(w_up[0]), "w_down": f(w_down[0]), "w_ple_gate": f(w_ple_gate[0]),
        "w_ple_proj": f(w_ple_proj[0]), "gcols": gcols, "gvec": gvec, "lamv": lamv, "ident": ident, "rotT": rotT, "t2": t2,
    }
    in_maps = []
    for core in range(8):
        b, j = core // 4, core % 4
        pos = (np.arange(S_ALL) + OWN * j) % S_ALL
        m = dict(shared)
        m["xr"] = np.ascontiguousarray(x[b][pos])
        m["pr"] = np.ascontiguousarray(p[0, b, OWN * j:OWN * (j + 1)])
        m["ropec"] = np.ascontiguousarray(np.tile(cos[pos].T, (4, 1)))
        m["ropes"] = np.ascontiguousarray(np.tile(sin[pos].T, (4, 1)))
        m["rowmask"] = _rowmask(j)
        in_maps.append(m)
    res = run_bass_kernel_spmd(nc, in_maps, core_ids=list(range(8)))
    _NC_CACHE["last"] = res
    outp = np.empty((2, S_ALL, D), np.float32)
    for core in range(8):
        b, j = core // 4, core % 4
        outp[b, OWN * j:OWN * (j + 1)] = res.results[core]["out"]
    return outp
```
